# Optimizing a Trainium2 kernel written in Bass

```python
import math
import jax
import jax.numpy as jnp
from jax import lax
import numpy as np

D_MODEL = 4096
BATCH = 8
SEQ = 2048
DEPTH = 2

GRID_W = 64
CTX_LEN = 256
N_BRANCH = 4
BRANCH_W = D_MODEL // N_BRANCH
HEAD_DIM = 128
N_HEADS = BRANCH_W // HEAD_DIM
DIFF_DIM = HEAD_DIM // 2
HALF_FF = (5 * D_MODEL) // 4
MOD_CHUNKS = 9
ROPE_THETA = 10000.0
Q_BLOCK = 128
CHUNK = 64
CONV_W = 3
HY_EMB = 33
HY_ORDER = 64
HY_FAST_DECAY = 0.3
HY_SLOW_DECAY = 1.5
HY_TARGET = 1e-2
EPS = 1e-6
MASK_NEG = -1e30
LB_FLOOR = 1e-20
IN_SPLITS = (
    BRANCH_W, BRANCH_W, BRANCH_W,
    BRANCH_W, BRANCH_W, BRANCH_W, BRANCH_W, 2 * N_HEADS, 2 * N_HEADS,
    BRANCH_W, 2 * BRANCH_W, BRANCH_W, BRANCH_W,
    BRANCH_W, BRANCH_W, BRANCH_W,
)
IN_COLS = 15 * BRANCH_W + 4 * N_HEADS

kernel_name = 'hybrid_diffattn_gdn_hgrn2_hyena_prefix'


def rms_norm(x, g):
    xf = x.astype(jnp.float32)
    y = xf * lax.rsqrt(jnp.mean(xf * xf, axis=-1, keepdims=True) + EPS)
    return (y * g.astype(jnp.float32)).astype(x.dtype)


def _l2norm(x):
    return x * lax.rsqrt(jnp.sum(x * x, axis=-1, keepdims=True) + EPS)


def _heads(t):
    B, L, W = t.shape
    return jnp.moveaxis(t.astype(jnp.float32).reshape(B, L, W // HEAD_DIM, HEAD_DIM), 1, 2)


def _merge_heads(o):
    B, H, L, hd = o.shape
    return jnp.moveaxis(o, 1, 2).reshape(B, L, H * hd)


def _gated_head_norm(o, z, g):
    return _merge_heads(rms_norm(o, g) * jax.nn.silu(_heads(z)))


def _modulate(h, mod, i):
    return h * (1.0 + mod[:, :, 3 * i + 1]) + mod[:, :, 3 * i]


def _half_ffn(h, mod, i, g, w_i, w_o):
    hn = _modulate(rms_norm(h, g), mod, i)
    gate, up = jnp.split(hn @ w_i, 2, axis=-1)
    return h + 0.5 * mod[:, :, 3 * i + 2] * ((jax.nn.silu(gate) * up) @ w_o)


def _split_cols(t):
    cuts = []
    acc = 0
    for s in IN_SPLITS[:-1]:
        acc += s
        cuts.append(acc)
    return jnp.split(t, cuts, axis=-1)


def _centred_conv(x, w):
    width = w.shape[0]
    pad = width // 2
    L = x.shape[1]
    xp = jnp.pad(x, ((0, 0), (pad, pad), (0, 0)))
    return sum(xp[:, j:j + L] * w[j] for j in range(width))


def axial_rope(rows):
    n_freq = DIFF_DIM // 4
    inv = ROPE_THETA ** (-jnp.arange(n_freq, dtype=jnp.float32) / n_freq)
    r = jnp.repeat(jnp.arange(rows, dtype=jnp.float32), GRID_W)
    col = jnp.tile(jnp.arange(GRID_W, dtype=jnp.float32), rows)
    ang = jnp.concatenate([r[:, None] * inv, col[:, None] * inv], axis=-1)
    return jnp.cos(ang), jnp.sin(ang)


def _rope(t, cos, sin):
    cos = cos[None, :, None, None]
    sin = sin[None, :, None, None]
    t1, t2 = jnp.split(t, 2, axis=-1)
    return jnp.concatenate([t1 * cos - t2 * sin, t1 * sin + t2 * cos], axis=-1)


def _diff_softmax(q, k, v, lam):
    s = jnp.einsum('bhqcd,bhkcd->bhcqk', q, k) * DIFF_DIM ** -0.5
    p = jax.nn.softmax(s, axis=-1)
    return jnp.einsum('bhqk,bhkv->bhqv', p[:, :, 0] - lam * p[:, :, 1], v)


def diff_attention(px, pe, cos, sin, qk_g, lam_p, subln_g, lam_init, need_ctx):
    qx, kx, vx = px
    qe, ke, ve = pe
    B, L, _ = qx.shape

    def qk(t, g):
        t = t.astype(jnp.float32).reshape(t.shape[0], t.shape[1], N_HEADS, 2, DIFF_DIM)
        return rms_norm(t, g)

    qx = jnp.moveaxis(_rope(qk(qx, qk_g[0]), cos, sin), 1, 2)
    kx = jnp.moveaxis(_rope(qk(kx, qk_g[1]), cos, sin), 1, 2)
    ke = jnp.moveaxis(qk(ke, qk_g[1]), 1, 2)
    vx = _heads(vx)
    ve = _heads(ve)
    lam_p = lam_p.astype(jnp.float32)
    lam = jnp.exp(jnp.sum(lam_p[0] * lam_p[1])) - jnp.exp(jnp.sum(lam_p[2] * lam_p[3])) + lam_init
    k_all = jnp.concatenate([kx, ke], axis=2)
    v_all = jnp.concatenate([vx, ve], axis=2)
    nb = L // Q_BLOCK
    qb = jnp.moveaxis(qx.reshape(B, N_HEADS, nb, Q_BLOCK, 2, DIFF_DIM), 2, 0)
    ob = lax.map(lambda qq: _diff_softmax(qq, k_all, v_all, lam), qb)
    ox = jnp.moveaxis(ob, 0, 2).reshape(B, N_HEADS, L, HEAD_DIM)
    out_x = _merge_heads(rms_norm(ox, subln_g) * (1.0 - lam_init))
    out_e = None
    if need_ctx:
        qe = jnp.moveaxis(qk(qe, qk_g[0]), 1, 2)
        oe = _diff_softmax(qe, ke, ve, lam)
        out_e = _merge_heads(rms_norm(oe, subln_g) * (1.0 - lam_init))
    return out_x, out_e


def _unit_lower_inverse(a):
    C = a.shape[-1]
    n = -a
    p = jnp.eye(C, dtype=a.dtype) + n
    m = n
    for _ in range(int(math.log2(C)) - 1):
        m = m @ m
        p = p + p @ m
    return p


def _gdn_chunked(q, k, v, beta, g, s0):
    B, H, L, dk = q.shape
    dv = v.shape[-1]
    n = L // CHUNK
    q = q.reshape(B, H, n, CHUNK, dk)
    k = k.reshape(B, H, n, CHUNK, dk)
    v = v.reshape(B, H, n, CHUNK, dv)
    beta = beta.reshape(B, H, n, CHUNK)
    gc = jnp.cumsum(g.reshape(B, H, n, CHUNK), axis=-1)
    idx = jnp.arange(CHUNK)
    incl = idx[:, None] >= idx[None, :]
    strict = idx[:, None] > idx[None, :]
    decay = jnp.exp(jnp.where(incl, gc[..., :, None] - gc[..., None, :], MASK_NEG))
    kb = k * beta[..., None]
    a = jnp.where(strict, jnp.einsum('bhnid,bhnjd->bhnij', kb, k) * decay, 0.0)
    t = _unit_lower_inverse(a)
    u = t @ (v * beta[..., None])
    w = t @ (kb * jnp.exp(gc)[..., None])
    attn = jnp.einsum('bhnid,bhnjd->bhnij', q, k) * decay
    qg = q * jnp.exp(gc)[..., None]
    kg = k * jnp.exp(gc[..., -1:] - gc)[..., None]
    dl = jnp.exp(gc[..., -1])
    xs = tuple(jnp.moveaxis(z, 2, 0) for z in (u, w, attn, qg, kg, dl))

    def step(s, inp):
        u_i, w_i, a_i, qg_i, kg_i, d_i = inp
        v_new = u_i - w_i @ s
        o_i = qg_i @ s + a_i @ v_new
        s = s * d_i[..., None, None] + jnp.swapaxes(kg_i, -1, -2) @ v_new
        return s, o_i

    s, o = lax.scan(step, s0, xs)
    return jnp.moveaxis(o, 0, 2).reshape(B, H, L, dv), s


def _hgrn_chunked(q, k, v, g, s0):
    B, H, L, dk = q.shape
    dv = v.shape[-1]
    n = L // CHUNK
    q, k, g = (z.reshape(B, H, n, CHUNK, dk) for z in (q, k, g))
    v = v.reshape(B, H, n, CHUNK, dv)
    gc = jnp.cumsum(g, axis=3)
    qg = q * jnp.exp(gc)
    kg = k * jnp.exp(gc[..., -1:, :] - gc)
    dl = jnp.exp(gc[..., -1, :])
    idx = jnp.arange(CHUNK)
    incl = (idx[:, None] >= idx[None, :])[:, :, None]
    xs = tuple(jnp.moveaxis(z, 2, 0) for z in (q, k, v, gc, qg, kg, dl))

    def step(s, inp):
        q_i, k_i, v_i, gc_i, qg_i, kg_i, d_i = inp
        rel = jnp.exp(jnp.where(incl, gc_i[:, :, :, None, :] - gc_i[:, :, None, :, :], MASK_NEG))
        a_i = jnp.einsum('bhik,bhjk,bhijk->bhij', q_i, k_i, rel)
        o_i = qg_i @ s + a_i @ v_i
        s = s * d_i[..., None] + jnp.swapaxes(kg_i, -1, -2) @ v_i
        return s, o_i

    s, o = lax.scan(step, s0, xs)
    return jnp.moveaxis(o, 0, 2).reshape(B, H, L, dv), s


def _two_way(run, ctx_dirs, lat_dirs, s0):
    outs_c, outs_l = [], []
    for d in range(2):
        ca, la = ctx_dirs[d], lat_dirs[d]
        if d == 1:
            ca = tuple(jnp.flip(a, axis=2) for a in ca)
            la = tuple(jnp.flip(a, axis=2) for a in la)
        oc, sc = run(*ca, s0)
        ol, _ = run(*la, sc)
        if d == 1:
            oc, ol = jnp.flip(oc, axis=2), jnp.flip(ol, axis=2)
        outs_c.append(oc)
        outs_l.append(ol)
    return outs_c[0] + outs_c[1], outs_l[0] + outs_l[1]


def _gdn_prep(q, k, v, a, b, conv_w, a_log, dt_bias):
    B, L, _ = q.shape
    qkv = jax.nn.silu(_centred_conv(jnp.concatenate([q, k, v], axis=-1), conv_w)).astype(jnp.float32)
    q, k, v = jnp.split(qkv, 3, axis=-1)
    q = _l2norm(_heads(q)) * HEAD_DIM ** -0.5
    k = _l2norm(_heads(k))
    v = _heads(v)
    a = a.astype(jnp.float32).reshape(B, L, 2, N_HEADS)
    b = b.astype(jnp.float32).reshape(B, L, 2, N_HEADS)
    g = -jnp.exp(a_log.astype(jnp.float32)) * jax.nn.softplus(a + dt_bias.astype(jnp.float32))
    beta = jax.nn.sigmoid(b)
    return tuple((q, k, v, jnp.moveaxis(beta[:, :, d], 1, 2), jnp.moveaxis(g[:, :, d], 1, 2))
                 for d in range(2))


def gated_deltanet(px, pe, conv_w, a_log, dt_bias, norm_g, need_ctx):
    lat = _gdn_prep(px[0], px[1], px[2], px[4], px[5], conv_w, a_log, dt_bias)
    con = _gdn_prep(pe[0], pe[1], pe[2], pe[4], pe[5], conv_w, a_log, dt_bias)
    s0 = jnp.zeros((px[0].shape[0], N_HEADS, HEAD_DIM, HEAD_DIM), jnp.float32)
    o_e, o_x = _two_way(_gdn_chunked, con, lat, s0)
    out_x = _gated_head_norm(o_x, px[3], norm_g)
    out_e = _gated_head_norm(o_e, pe[3], norm_g) if need_ctx else None
    return out_x, out_e


def _hgrn_prep(hq, hf, hi, lb):
    B, L, _ = hq.shape
    q = _heads(hq)
    v = _heads(hi)
    f = hf.astype(jnp.float32).reshape(B, L, 2, N_HEADS, HEAD_DIM)
    lbh = lb.reshape(2, N_HEADS, HEAD_DIM)
    logf = jnp.logaddexp(jnp.log(jnp.maximum(lbh, LB_FLOOR)), jnp.log1p(-lbh) + jax.nn.log_sigmoid(f))
    k = (1.0 - lbh) * jax.nn.sigmoid(-f)
    return tuple((q, jnp.moveaxis(k[:, :, d], 1, 2), v, jnp.moveaxis(logf[:, :, d], 1, 2))
                 for d in range(2))


def hgrn2(px, pe, lb, norm_g, need_ctx):
    lat = _hgrn_prep(px[0], px[1], px[2], lb)
    con = _hgrn_prep(pe[0], pe[1], pe[2], lb)
    s0 = jnp.zeros((px[0].shape[0], N_HEADS, HEAD_DIM, HEAD_DIM), jnp.float32)
    o_e, o_x = _two_way(_hgrn_chunked, con, lat, s0)
    out_x = _gated_head_norm(o_x, px[3], norm_g)
    out_e = _gated_head_norm(o_e, pe[3], norm_g) if need_ctx else None
    return out_x, out_e


def _hyena_filter(L, w1, b1, w2, b2, w3, freq):
    f32 = jnp.float32
    bands = (HY_EMB - 1) // 2
    t = jnp.linspace(0.0, 1.0, L, dtype=f32)[:, None]
    wpos = 2.0 * math.pi * jnp.arange(L, dtype=f32)[:, None] / L
    fb = jnp.linspace(1e-4, bands - 1, bands, dtype=f32)[None]
    z = jnp.concatenate([t, jnp.cos(fb * wpos), -jnp.sin(fb * wpos)], axis=-1)
    freq = freq.astype(f32)
    h = jnp.sin(freq[0] * (z @ w1.astype(f32) + b1.astype(f32)))
    h = jnp.sin(freq[1] * (h @ w2.astype(f32) + b2.astype(f32)))
    h = h @ w3.astype(f32)
    deltas = jnp.abs(jnp.linspace(math.log(HY_TARGET) / HY_SLOW_DECAY,
                                  math.log(HY_TARGET) / HY_FAST_DECAY, BRANCH_W, dtype=f32))
    h = h * jnp.exp(-t * jnp.tile(deltas, 2))
    h_fwd, h_bwd = jnp.split(h, 2, axis=-1)
    return jnp.concatenate([h_fwd, jnp.zeros((1, BRANCH_W), f32), h_bwd[:L - 1][::-1]], axis=0)


def _fft_conv(u, kern):
    L = u.shape[1]
    n = 2 * L
    y = jnp.fft.irfft(jnp.fft.rfft(u, n=n, axis=1) * jnp.fft.rfft(kern, n=n, axis=0), n=n, axis=1)
    return y[:, :L]


def hyena(p, conv_w, conv_b, w1, b1, w2, b2, w3, freq, bias):
    B, L, _ = p[0].shape
    u = _centred_conv(jnp.concatenate(p, axis=-1), conv_w) + conv_b
    v, x0, x1 = jnp.split(u.astype(jnp.float32), 3, axis=-1)
    vg = v * x1
    kern = _hyena_filter(L, w1, b1, w2, b2, w3, freq)
    y = _fft_conv(vg, kern) + vg * bias.astype(jnp.float32)
    return x0 * y


def _merge(xn, branches, w_gate, w_up, w_out):
    acc = 0.0
    for i, o in enumerate(branches):
        acc = acc + jax.nn.sigmoid(xn @ w_gate[i]) * (o.astype(xn.dtype) @ w_up[i])
    return acc @ w_out


def setup_inputs(seed: int = 0) -> dict:
    key = jax.random.key(seed)
    ks = jax.random.split(key, 32)
    f32 = jnp.float32
    D = D_MODEL

    def nrm(i, shape, s):
        return jax.random.normal(ks[i], shape, f32) * s

    dt = jnp.exp(jax.random.uniform(ks[15], (DEPTH, 2, N_HEADS), f32, math.log(1e-3), math.log(1e-1)))
    return {
        'x': nrm(0, (BATCH, SEQ, D), 1.0),
        'c': nrm(1, (BATCH, D), 1.0),
        'ctx': nrm(2, (BATCH, CTX_LEN, D), 1.0),
        'c_ctx': nrm(3, (D,), 1.0),
        'norm_g': 1.0 + nrm(4, (DEPTH, 3, D), 0.02),
        'w_mod': nrm(5, (DEPTH, D, MOD_CHUNKS * D), 0.5 * D ** -0.5),
        'b_mod': nrm(6, (DEPTH, MOD_CHUNKS * D), 0.01),
        'ffn_w_in': nrm(7, (DEPTH, 2, D, 2 * HALF_FF), D ** -0.5),
        'ffn_w_out': nrm(8, (DEPTH, 2, HALF_FF, D), HALF_FF ** -0.5),
        'w_in': nrm(9, (DEPTH, D, IN_COLS), D ** -0.5),
        'attn_qk_g': 1.0 + nrm(10, (DEPTH, 2, DIFF_DIM), 0.02),
        'attn_lambda': nrm(11, (DEPTH, 4, DIFF_DIM), 0.1),
        'attn_subln_g': 1.0 + nrm(12, (DEPTH, HEAD_DIM), 0.02),
        'gdn_conv_w': nrm(13, (DEPTH, CONV_W, 3 * BRANCH_W), CONV_W ** -0.5),
        'gdn_a_log': jnp.log(jax.random.uniform(ks[14], (DEPTH, 2, N_HEADS), f32, 1.0, 16.0)),
        'gdn_dt_bias': dt + jnp.log(-jnp.expm1(-dt)),
        'gdn_norm_g': 1.0 + nrm(16, (DEPTH, HEAD_DIM), 0.02),
        'hg_lb_logits': nrm(17, (DEPTH, 2, BRANCH_W), 0.5),
        'hg_norm_g': 1.0 + nrm(18, (DEPTH, HEAD_DIM), 0.02),
        'hy_conv_w': nrm(19, (DEPTH, CONV_W, 3 * BRANCH_W), CONV_W ** -0.5),
        'hy_conv_b': nrm(20, (DEPTH, 3 * BRANCH_W), 0.01),
        'hy_w1': nrm(21, (DEPTH, HY_EMB, HY_ORDER), HY_EMB ** -0.5),
        'hy_b1': nrm(22, (DEPTH, HY_ORDER), 0.1),
        'hy_w2': nrm(23, (DEPTH, HY_ORDER, HY_ORDER), HY_ORDER ** -0.5),
        'hy_b2': nrm(24, (DEPTH, HY_ORDER), 0.1),
        'hy_w3': nrm(25, (DEPTH, HY_ORDER, 2 * BRANCH_W), 0.05 * HY_ORDER ** -0.5),
        'hy_freq': 1.0 + nrm(26, (DEPTH, 2, HY_ORDER), 0.01),
        'hy_bias': nrm(27, (DEPTH, BRANCH_W), 0.1),
        'w_gate': nrm(28, (DEPTH, N_BRANCH, D, D), D ** -0.5),
        'w_up': nrm(29, (DEPTH, N_BRANCH, BRANCH_W, D), BRANCH_W ** -0.5),
        'w_out': nrm(30, (DEPTH, D, D), D ** -0.5),
    }


def reference(x, c, ctx, c_ctx, norm_g, w_mod, b_mod, ffn_w_in, ffn_w_out, w_in,
              attn_qk_g, attn_lambda, attn_subln_g,
              gdn_conv_w, gdn_a_log, gdn_dt_bias, gdn_norm_g,
              hg_lb_logits, hg_norm_g,
              hy_conv_w, hy_conv_b, hy_w1, hy_b1, hy_w2, hy_b2, hy_w3, hy_freq, hy_bias,
              w_gate, w_up, w_out):
    rows = x.shape[1] // GRID_W
    cos, sin = axial_rope(rows)
    p_lb = jax.nn.softmax(hg_lb_logits.astype(jnp.float32), axis=0)
    lb_all = jnp.cumsum(p_lb, axis=0) - p_lb[:1]
    e = ctx
    for l in range(DEPTH):
        last = l == DEPTH - 1
        mod_x = (jax.nn.silu(c) @ w_mod[l] + b_mod[l]).reshape(c.shape[0], 1, MOD_CHUNKS, D_MODEL)
        mod_e = (jax.nn.silu(c_ctx)[None] @ w_mod[l] + b_mod[l]).reshape(1, 1, MOD_CHUNKS, D_MODEL)
        x = _half_ffn(x, mod_x, 0, norm_g[l, 0], ffn_w_in[l, 0], ffn_w_out[l, 0])
        e = _half_ffn(e, mod_e, 0, norm_g[l, 0], ffn_w_in[l, 0], ffn_w_out[l, 0])
        xn = _modulate(rms_norm(x, norm_g[l, 1]), mod_x, 1)
        en = _modulate(rms_norm(e, norm_g[l, 1]), mod_e, 1)
        px = _split_cols(xn @ w_in[l])
        pe = _split_cols(en @ w_in[l])
        lam_init = 0.8 - 0.6 * math.exp(-0.3 * l)
        att_x, att_e = diff_attention(px[0:3], pe[0:3], cos, sin, attn_qk_g[l], attn_lambda[l],
                                      attn_subln_g[l], lam_init, not last)
        gdn_x, gdn_e = gated_deltanet(px[3:9], pe[3:9], gdn_conv_w[l], gdn_a_log[l],
                                      gdn_dt_bias[l], gdn_norm_g[l], not last)
        hg_x, hg_e = hgrn2(px[9:13], pe[9:13], lb_all[l], hg_norm_g[l], not last)
        hy_x = hyena(px[13:16], hy_conv_w[l], hy_conv_b[l], hy_w1[l], hy_b1[l], hy_w2[l],
                     hy_b2[l], hy_w3[l], hy_freq[l], hy_bias[l])
        x = x + mod_x[:, :, 5] * _merge(xn, (att_x, gdn_x, hg_x, hy_x), w_gate[l], w_up[l], w_out[l])
        x = _half_ffn(x, mod_x, 2, norm_g[l, 2], ffn_w_in[l, 1], ffn_w_out[l, 1])
        if not last:
            hy_e = hyena(pe[13:16], hy_conv_w[l], hy_conv_b[l], hy_w1[l], hy_b1[l], hy_w2[l],
                         hy_b2[l], hy_w3[l], hy_freq[l], hy_bias[l])
            e = e + mod_e[:, :, 5] * _merge(en, (att_e, gdn_e, hg_e, hy_e), w_gate[l], w_up[l], w_out[l])
            e = _half_ffn(e, mod_e, 2, norm_g[l, 2], ffn_w_in[l, 1], ffn_w_out[l, 1])
    return x
```

```python
import ml_dtypes
import numpy as np
import concourse.bass as bass
import concourse.mybir as mybir
from concourse.bass_utils import run_bass_kernel_spmd

F32 = mybir.dt.float32
BF16 = mybir.dt.bfloat16
ALU = mybir.AluOpType
AF = mybir.ActivationFunctionType
AX = mybir.AxisListType

ENGS = ['pe', 'act', 'dve', 'pool', 'sp']
NDMA = 16


class Trk:
    _inst = [0]

    def __init__(self, nc, stack):
        self.nc = nc
        Trk._inst[0] += 1
        u = '_%d' % Trk._inst[0]
        self.q = {e: [] for e in ENGS}
        self.sem = {e: stack.enter_context(nc.semaphore('s_' + e + u)) for e in ENGS}
        self.cnt = {e: 0 for e in ENGS}
        self.dsem = [stack.enter_context(nc.semaphore('d%d%s' % (i, u))) for i in range(NDMA)]
        self.csem = None
        self.ccnt = 0
        self.psem = stack.enter_context(nc.semaphore('s_phase' + u))
        self.gsem = stack.enter_context(nc.semaphore('s_go' + u))
        self.nreset = 0
        self.dcnt = [0] * NDMA
        self.dnext = 0
        self.seen = {e: {} for e in ENGS}
        self.lastw = {}
        self.readers = {}
        self.nops = 0

    def _semh(self, key):
        if key == 'cc':
            return self.csem
        return self.sem[key] if isinstance(key, str) else self.dsem[key[1]]

    def _need(self, e, tok):
        if tok is None:
            return
        key, val = tok
        if e == 'pe' and key == 'pe':
            return
        if self.seen[e].get(key, 0) >= val:
            return
        self.seen[e][key] = val
        self.q[e].append(('w', self._semh(key), val))

    def _deps(self, e, r, w):
        for x in list(r) + list(w):
            self._need(e, self.lastw.get(x))
        for x in w:
            for t in self.readers.get(x, ()):
                self._need(e, t)

    def _commit(self, tok, r, w):
        for x in r:
            self.readers.setdefault(x, []).append(tok)
        for x in w:
            self.lastw[x] = tok
            self.readers[x] = []

    def op(self, e, fn, r=(), w=(), sig=True):
        self._deps(e, r, w)
        if sig:
            self.cnt[e] += 1
            tok = (e, self.cnt[e])
            self.q[e].append(('i', fn, self.sem[e], 1))
        else:
            tok = (e, self.cnt[e] + 1)
            self.q[e].append(('i', fn, None, 0))
        self._commit(tok, r, w)
        self.nops += 1

    def dma(self, e, out, in_, r=(), w=(), **kw):
        i = self.dnext
        self.dnext = (self.dnext + 1) % NDMA
        if self.dcnt[i] > 0:
            self._need(e, (('d', i), 16 * self.dcnt[i]))
        self._deps(e, r, w)
        self.dcnt[i] += 1
        tok = (('d', i), 16 * self.dcnt[i])
        self.q[e].append(('i', (lambda eng, out=out, in_=in_, kw=kw: eng.dma_start(out=out, in_=in_, **kw)),
                          self.dsem[i], 16))
        self._commit(tok, r, w)
        self.nops += 1

    def allgather(self, out, in_, ncores, r=(), w=()):
        self._deps('pool', r, w)
        if self.ccnt > 0:
            self._need('pool', ('cc', self.ccnt))
        self.ccnt += 1
        tok = ('cc', self.ccnt)
        self.q['pool'].append(('i', (lambda eng, out=out, in_=in_: eng.collective_compute(
            "AllGather", ALU.bypass, replica_groups=[list(range(ncores))], ins=[in_], outs=[out])), self.csem, 1))
        self._commit(tok, r, w)
        self.nops += 1

    def matmul(self, ps, lhsT, rhs, start=True, stop=True, r=(), w=(), sig=True):
        self.op('pe', lambda e: e.matmul(ps, lhsT, rhs, start=start, stop=stop), r=r, w=w, sig=sig)

    def mm_raw(self, ps, lhsT, rhs, start=True, stop=True, r=(), w=(), sig=True):
        self.op('pe', lambda e: e.matmul(ps, lhsT, rhs, start=start, stop=stop, skip_group_check=True),
                r=r, w=w, sig=sig)

    def transpose(self, ps, in_, ident, r=(), w=(), sig=True):
        self.op('pe', lambda e: e.transpose(ps, in_, ident), r=r, w=w, sig=sig)

    def act(self, out, in_, func, bias=None, scale=1.0, accum_out=None, r=(), w=()):
        kw = {}
        if bias is not None:
            kw['bias'] = bias
        if accum_out is not None:
            kw['accum_out'] = accum_out
        self.op('act', lambda e: e.activation(out, in_, func, scale=scale, **kw), r=r, w=w)

    def tt(self, eng, out, in0, in1, op, r=(), w=()):
        self.op(eng, lambda e: e.tensor_tensor(out, in0, in1, op), r=r, w=w)

    def ts(self, eng, out, in0, s1, s2, op0, op1=None, r=(), w=()):
        if op1 is None:
            self.op(eng, lambda e: e.tensor_scalar(out, in0, s1, None, op0), r=r, w=w)
        else:
            self.op(eng, lambda e: e.tensor_scalar(out, in0, s1, s2, op0, op1), r=r, w=w)

    def stt(self, eng, out, in0, scalar, in1, op0, op1, r=(), w=()):
        self.op(eng, lambda e: e.scalar_tensor_tensor(out, in0, scalar, in1, op0, op1), r=r, w=w)

    def copy(self, eng, out, in_, r=(), w=()):
        if eng == 'act':
            self.op('act', lambda e: e.copy(out, in_), r=r, w=w)
        else:
            self.op(eng, lambda e: e.tensor_copy(out, in_), r=r, w=w)

    def recip(self, out, in_, r=(), w=()):
        self.op('dve', lambda e: e.reciprocal(out, in_), r=r, w=w)

    def memset(self, eng, ap, val, w=()):
        self.op(eng, lambda e: e.memset(ap, val), w=w)

    def barrier(self):
        toks = [(e, self.cnt[e]) for e in ENGS if self.cnt[e] > 0]
        toks += [(('d', i), 16 * self.dcnt[i]) for i in range(NDMA) if self.dcnt[i] > 0]
        if self.ccnt > 0:
            toks.append(('cc', self.ccnt))
        for e in ENGS:
            for t in toks:
                if t[0] != e:
                    self._need(e, t)
        self.lastw = {}
        self.readers = {}

    def hard_reset(self):
        self.barrier()
        self.nreset += 1
        k = self.nreset
        for e in ENGS:
            self.q[e].append(('inc', self.psem, 1))
        self.q['pool'].append(('w', self.psem, len(ENGS) * k))
        for e in ENGS:
            self.q['pool'].append(('clr', self.sem[e]))
        for s in self.dsem:
            self.q['pool'].append(('clr', s))
        self.q['pool'].append(('inc', self.gsem, 1))
        for e in ENGS:
            self.q[e].append(('w', self.gsem, k))
        self.cnt = {e: 0 for e in ENGS}
        self.dcnt = [0] * NDMA
        self.seen = {e: {} for e in ENGS}
        self.lastw = {}
        self.readers = {}

    def final_wait(self, e='sp'):
        for i in range(NDMA):
            if self.dcnt[i] > 0:
                self._need(e, (('d', i), 16 * self.dcnt[i]))
        for x in ENGS:
            if x != e and self.cnt[x] > 0:
                self._need(e, (x, self.cnt[x]))

    def flush(self):
        nc = self.nc
        q = self.q
        self.q = {e: [] for e in ENGS}

        def run(eng, items):
            for it in items:
                if it[0] == 'w':
                    eng.wait_ge(it[1], it[2])
                elif it[0] == 'inc':
                    eng.sem_inc(it[1], it[2])
                elif it[0] == 'clr':
                    eng.sem_clear(it[1])
                else:
                    ins = it[1](eng)
                    if it[2] is not None:
                        ins.then_inc(it[2], it[3])

        with nc.Block() as block:
            @block.tensor
            def _(eng):
                run(eng, q['pe'])

            @block.scalar
            def _(eng):
                run(eng, q['act'])

            @block.vector
            def _(eng):
                run(eng, q['dve'])

            @block.gpsimd
            def _(eng):
                run(eng, q['pool'])

            @block.sync
            def _(eng):
                run(eng, q['sp'])


from contextlib import ExitStack
import math
import numpy as np

DM = 4096
KC = DM // 128
NLAT = 2048
NCTX = 256
NTOK = NLAT + NCTX
HFF = 5120
NMOD = 9 * DM
INC = 15392
EPS = 1e-6
DEPTH = 2


def token_tiles(nlat=NLAT, nctx=NCTX):
    tt = [(t0, 512, 0) for t0 in range(0, nlat, 512)]
    tt += [(nlat + t0, min(512, nctx - t0), 1) for t0 in range(0, nctx, 512)]
    return tt


class Ctx:
    pass


def wload(T, eng, dst, blocks, c0, ncol, wkeys):
    for (row0, rows, ap) in blocks:
        v = ap.rearrange("(c p) n -> p c n", p=128)
        T.dma(eng, dst[:, row0 // 128:(row0 + rows) // 128, :], v[:, :, c0:c0 + ncol], w=wkeys)


_UID = [0]


def alloc(nc, st, name, shape, dt):
    _UID[0] += 1
    return st.enter_context(nc.sbuf_tensor('sb_%s_%d' % (name, _UID[0]), shape, dt))


def palloc(nc, st, name, shape, dt=F32):
    _UID[0] += 1
    return st.enter_context(nc.psum_tensor('ps_%s_%d' % (name, _UID[0]), shape, dt))


def phase_consts(C):
    nc, T, st = C.nc, C.T, C.pst
    C.ident = alloc(nc, st, 'ident', [128, 128], F32)
    C.ones_bf = alloc(nc, st, 'ones_bf', [128, 128], BF16)
    C.ones_f = alloc(nc, st, 'ones_f', [128, 128], F32)
    T.dma('sp', C.ident[:], C.d['ident'], w=['ident'])
    T.memset('dve', C.ones_bf[:], 1.0, w=['ones_bf'])
    T.memset('dve', C.ones_f[:], 1.0, w=['ones_f'])
    C.modcol = alloc(nc, st, 'modcol', [128, DEPTH * 2 * 288], F32)
    C.normg = alloc(nc, st, 'normg', [128, DEPTH * 3 * 32], F32)
    C.Acol = alloc(nc, st, 'Acol', [128, DEPTH * 3 * 2 * 32], F32)
    C.Gcol = alloc(nc, st, 'Gcol', [128, DEPTH * 3 * 2 * 32], F32)


def mc(C, l, w, chunk):
    o = (l * 2 + w) * 288 + chunk * 32
    return C.modcol[:, o:o + 32]


def acol(C, l, i, w):
    o = ((l * 3 + i) * 2 + w) * 32
    return C.Acol[:, o:o + 32]


def gcol(C, l, i, w):
    o = ((l * 3 + i) * 2 + w) * 32
    return C.Gcol[:, o:o + 32]


def phase_load_x(C):
    nc, T = C.nc, C.T
    T.barrier()
    with ExitStack() as st:
        xin = [alloc(nc, st, 'xin%d' % i, [128, DM], F32) for i in range(2)]
        xst = [alloc(nc, st, 'xst%d' % i, [128, KC, 128], F32) for i in range(2)]
        pt = [palloc(nc, st, 'pt%d' % i, [128, 4, 128]) for i in range(4)]
        xTv = C.d['xT'].rearrange("(c p) t -> p c t", p=128)
        nt = 0
        np_ = 0
        for (src, n, off) in (('x', NLAT, 0), ('ctx', NCTX, NLAT)):
            for t0 in range(0, n, 128):
                b = nt % 2
                nt += 1
                T.dma('sp', xin[b][:], C.d[src][t0:t0 + 128, :], w=[('xin', b)])
                for c4 in range(KC // 4):
                    p = np_ % 4
                    np_ += 1
                    for j in range(4):
                        c = c4 * 4 + j
                        T.transpose(pt[p][:, j, :], xin[b][:, c * 128:(c + 1) * 128], C.ident[:],
                                    r=[('xin', b), 'ident'], w=[('pt', p)], sig=(j == 3))
                    eng = 'dve' if c4 % 2 == 0 else 'act'
                    T.copy(eng, xst[b][:, c4 * 4:(c4 + 1) * 4, :], pt[p][:], r=[('pt', p)], w=[('xst', b)])
                T.dma('sp', xTv[:, :, off + t0:off + t0 + 128], xst[b][:], r=[('xst', b)], w=[('xT', 'all')])
        T.flush()


def phase_store_out(C, dbg_ctx=False):
    nc, T = C.nc, C.T
    T.barrier()
    with ExitStack() as st:
        xin = [alloc(nc, st, 'oin%d' % i, [128, KC, 128], F32) for i in range(2)]
        xst = [alloc(nc, st, 'ost%d' % i, [128, DM], F32) for i in range(2)]
        pt = [palloc(nc, st, 'opt%d' % i, [128, 4, 128]) for i in range(4)]
        xTv = C.d['xT'].rearrange("(c p) t -> p c t", p=128)
        nt = 0
        np_ = 0
        jobs = [('out', NLAT, 0)]
        if dbg_ctx:
            jobs.append(('out_e', NCTX, NLAT))
        for (dst, n, off) in jobs:
            for t0 in range(0, n, 128):
                b = nt % 2
                nt += 1
                T.dma('sp', xin[b][:], xTv[:, :, off + t0:off + t0 + 128], r=[('xT', 'all')], w=[('oin', b)])
                for c4 in range(KC // 4):
                    p = np_ % 4
                    np_ += 1
                    for j in range(4):
                        c = c4 * 4 + j
                        T.transpose(pt[p][:, j, :], xin[b][:, c, :], C.ident[:],
                                    r=[('oin', b), 'ident'], w=[('opt', p)], sig=(j == 3))
                    eng = 'dve' if c4 % 2 == 0 else 'act'
                    T.copy(eng, xst[b][:, c4 * 512:(c4 + 1) * 512], pt[p][:].rearrange("p a b -> p (a b)"),
                           r=[('opt', p)], w=[('ost', b)])
                T.dma('sp', C.d[dst][t0:t0 + 128, :], xst[b][:], r=[('ost', b)], w=[(dst, t0)])
        T.flush()


def phase_mod(C):
    nc, T = C.nc, C.T
    T.barrier()
    with ExitStack() as st:
        cs = alloc(nc, st, 'cs', [128, KC, 33], F32)
        cv = alloc(nc, st, 'cv', [64, 128], F32)
        ng = alloc(nc, st, 'ng', [96, 128], F32)
        one33 = alloc(nc, st, 'one33', [33, 1], F32)
        bm = [alloc(nc, st, 'bm%d' % i, [33, 512], F32) for i in range(2)]
        mr = [alloc(nc, st, 'mr%d' % i, [33, 512], F32) for i in range(2)]
        wt = [alloc(nc, st, 'wt%d' % i, [128, KC, 512], F32) for i in range(2)]
        pcv = palloc(nc, st, 'pcv', [128, 128])
        pm = [palloc(nc, st, 'pm%d' % i, [33, 512]) for i in range(2)]
        pc = [palloc(nc, st, 'pc%d' % i, [128, 8]) for i in range(2)]
        T.memset('dve', cs[:], 0.0, w=['cs'])
        T.memset('dve', one33[:], 1.0, w=['one33'])
        for i in range(2):
            T.memset('dve', bm[i][:], 0.0, w=[('bm', i)])
        T.dma('sp', cv[:], C.d['cvec'], w=['cv'])
        T.transpose(pcv[:, 0:64], cv[:], C.ident[0:64, 0:64], r=['cv', 'ident'], w=['pcv'])
        T.act(cs[:, :, 0], pcv[:, 0:32], AF.Silu, r=['pcv'], w=['cs'])
        T.act(cs[:, :, 32], pcv[:, 32:64], AF.Silu, r=['pcv'], w=['cs'])
        for l in range(DEPTH):
            T.dma('sp', ng[:], C.d['norm_g'][l * 96:(l + 1) * 96, :], w=['ng'])
            T.transpose(pcv[:, 0:96], ng[:], C.ident[0:96, 0:96], r=['ng', 'ident'], w=['pcv'])
            T.copy('dve', C.normg[:, l * 96:(l + 1) * 96], pcv[:, 0:96], r=['pcv'], w=['normg'])
        nb = 0
        for l in range(DEPTH):
            for blk in range(NMOD // 512):
                b = nb % 2
                nb += 1
                n0 = blk * 512
                wload(T, 'sp', wt[b][:], C.W['w_mod'][l], n0, 512, [('wt', b)])
                T.dma('act', bm[b][0:1, :], C.d['b_mod'][l:l + 1, n0:n0 + 512], w=[('bm', b)])
                T.dma('act', bm[b][32:33, :], C.d['b_mod'][l:l + 1, n0:n0 + 512], w=[('bm', b)])
                for kc in range(KC):
                    T.matmul(pm[b][:], cs[:, kc, :], wt[b][:, kc, :], start=(kc == 0), stop=(kc == KC - 1),
                             r=['cs', ('wt', b)], w=[('pm', b)], sig=(kc == KC - 1))
                T.tt('dve', mr[b][:], pm[b][:], bm[b][:], ALU.add, r=[('pm', b), ('bm', b)], w=[('mr', b)])
                for w_ in range(2):
                    for c in range(4):
                        T.matmul(pc[b][:, w_ * 4 + c:w_ * 4 + c + 1], mr[b][32 * w_:32 * w_ + 1, c * 128:(c + 1) * 128],
                                 one33[32 * w_:32 * w_ + 1, 0:1], r=[('mr', b), 'one33'], w=[('pc', b)],
                                 sig=(w_ == 1 and c == 3))
                base = l * 2 * 288
                outv = C.modcol[:, base:base + 576].rearrange("p (w j) -> p w j", w=2)[:, :, blk * 4:(blk + 1) * 4]
                T.copy('act', outv, pc[b][:].rearrange("p (w c) -> p w c", w=2), r=[('pc', b)], w=['modcol'])
        for l in range(DEPTH):
            for i in range(3):
                g = C.normg[:, (l * 3 + i) * 32:(l * 3 + i + 1) * 32]
                for w_ in range(2):
                    T.stt('dve', acol(C, l, i, w_), mc(C, l, w_, 3 * i + 1), 1.0, g, ALU.add, ALU.mult,
                          r=['modcol', 'normg'], w=['Acol'])
                    T.ts('dve', gcol(C, l, i, w_), mc(C, l, w_, 3 * i + 2), 1.0 if i == 1 else 0.5, None, ALU.mult,
                         r=['modcol'], w=['Gcol'])
        T.flush()


def phase_norm(C, l, i, tiles=None):
    nc, T = C.nc, C.T
    T.barrier()
    tiles = tiles if tiles is not None else token_tiles()
    tiles = [(t0 + s, min(256, n - s), w_) for (t0, n, w_) in tiles for s in range(0, n, 256)]
    with ExitStack() as st:
        xs = [alloc(nc, st, 'xs%d' % k, [128, KC, 256], F32) for k in range(2)]
        xn = [alloc(nc, st, 'xn%d' % k, [128, KC, 256], BF16) for k in range(2)]
        sq = [alloc(nc, st, 'sq%d' % k, [128, 256], BF16) for k in range(4)]
        tmp = [alloc(nc, st, 'tmp%d' % k, [128, 256], F32) for k in range(4)]
        rstd = [alloc(nc, st, 'rstd%d' % k, [128, 256], F32) for k in range(2)]
        pss = [palloc(nc, st, 'pss%d' % k, [128, 256]) for k in range(2)]
        xTv = C.d['xT'].rearrange("(c p) t -> p c t", p=128)
        xnv = C.d['xnT'].rearrange("(c p) t -> p c t", p=128)
        nq = 0
        for ti, (t0, n, w_) in enumerate(tiles):
            b = ti % 2
            T.dma('sp', xs[b][:, :, 0:n], xTv[:, :, t0:t0 + n], r=[('xT', 'all')], w=[('xs', b)])
            for c in range(KC):
                q = nq % 4
                nq += 1
                T.act(sq[q][:, 0:n], xs[b][:, c, 0:n], AF.Square, r=[('xs', b)], w=[('sq', q)])
                T.matmul(pss[b][:, 0:n], C.ones_bf[:], sq[q][:, 0:n], start=(c == 0), stop=(c == KC - 1),
                         r=[('sq', q), 'ones_bf'], w=[('pss', b)], sig=True)
            T.ts('dve', rstd[b][:, 0:n], pss[b][:, 0:n], 1.0 / DM, EPS, ALU.mult, ALU.add, r=[('pss', b)], w=[('rstd', b)])
            T.act(rstd[b][:, 0:n], rstd[b][:, 0:n], AF.Sqrt, r=[('rstd', b)], w=[('rstd', b)])
            T.recip(rstd[b][:, 0:n], rstd[b][:, 0:n], r=[('rstd', b)], w=[('rstd', b)])
            A = acol(C, l, i, w_)
            SH = mc(C, l, w_, 3 * i)
            for c in range(KC):
                q = nq % 4
                nq += 1
                T.stt('dve', tmp[q][:, 0:n], xs[b][:, c, 0:n], A[:, c:c + 1], rstd[b][:, 0:n], ALU.mult, ALU.mult,
                      r=[('xs', b), 'Acol', ('rstd', b)], w=[('tmp', q)])
                T.act(xn[b][:, c, 0:n], tmp[q][:, 0:n], AF.Identity, bias=SH[:, c:c + 1],
                      r=[('tmp', q), 'modcol'], w=[('xn', b)])
            T.dma('sp', xnv[:, :, t0:t0 + n], xn[b][:, :, 0:n], r=[('xn', b)], w=[('xnT', 'all')])
        T.flush()


def linear_fm(C, name, W, K, slabs, in_d, consume, tiles=None, after_tile=None, nps=4, wcols=512):
    nc, T = C.nc, C.T
    tiles = tiles if tiles is not None else token_tiles()
    kc_n = K // 128
    with ExitStack() as st:
        wsl = [alloc(nc, st, name + '_w%d' % k, [128, kc_n, wcols], BF16) for k in range(2)]
        inb = [alloc(nc, st, name + '_i%d' % k, [128, kc_n, 512], BF16) for k in range(2)]
        ps = [palloc(nc, st, name + '_p%d' % k, [128, 512]) for k in range(nps)]
        inv = in_d.rearrange("(c p) t -> p c t", p=128)
        nin = 0
        npp = 0
        for si, slab in enumerate(slabs):
            wb = si % 2
            o = 0
            for (c0, ncol) in slab:
                wload(T, 'pool', wsl[wb][:, :, o:o + ncol], W, c0, ncol, [(name + 'w', wb)])
                o += ncol
            chunks = [(c, min(128, o - c)) for c in range(0, o, 128)]
            for ti, (t0, n, w_) in enumerate(tiles):
                ib = nin % 2
                nin += 1
                T.dma('sp', inb[ib][:, :, 0:n], inv[:, :, t0:t0 + n], r=[(in_d.name, 'all')], w=[(name + 'i', ib)])
                for hi, (co, cw) in enumerate(chunks):
                    p = npp % nps
                    npp += 1
                    for kc in range(kc_n):
                        T.matmul(ps[p][0:cw, 0:n], wsl[wb][:, kc, co:co + cw], inb[ib][:, kc, 0:n],
                                 start=(kc == 0), stop=(kc == kc_n - 1),
                                 r=[(name + 'w', wb), (name + 'i', ib)], w=[(name + 'p', p)], sig=(kc == kc_n - 1))
                    consume(si, hi, ti, (t0, n, w_), ps[p][0:cw, 0:n], (name + 'p', p))
                if after_tile is not None:
                    after_tile(si, ti, (t0, n, w_))
        T.flush()


def phase_ffn(C, l, i, fi, tiles=None):
    nc, T = C.nc, C.T
    tiles = tiles if tiles is not None else token_tiles()
    phase_norm(C, l, i, tiles)
    T.barrier()
    W1 = C.W['ffn_w_in'][l * 2 + fi]
    W2 = C.W['ffn_w_out'][l * 2 + fi]
    with ExitStack() as st:
        sg = [alloc(nc, st, 'sg%d' % k, [128, 512], F32) for k in range(2)]
        ab = [alloc(nc, st, 'ab%d' % k, [128, 2, 512], BF16) for k in range(2)]
        acv = C.d['actT'].rearrange("(c p) t -> p c t", p=128)
        state = {'g': None, 'n': 0}
        slabs = [[(s * 256, 256), (HFF + s * 256, 256)] for s in range(HFF // 256)]

        def consume(si, hi, ti, tile, ps, pskey):
            t0, n, w_ = tile
            if hi < 2:
                state.setdefault('gates', {})[hi] = (ps, pskey)
                return
            h = hi - 2
            gps, gkey = state['gates'][h]
            k = state['n'] % 2
            state['n'] += 1
            abk = (si * len(tiles) + ti) % 2
            T.act(sg[k][:, 0:n], gps, AF.Silu, r=[gkey], w=[('sg', k)])
            T.tt('dve', ab[abk][:, h, 0:n], sg[k][:, 0:n], ps, ALU.mult, r=[('sg', k), pskey], w=[('ab', abk)])

        def after_tile(si, ti, tile):
            t0, n, w_ = tile
            abk = (si * len(tiles) + ti) % 2
            T.dma('sp', acv[:, si * 2:si * 2 + 2, t0:t0 + n], ab[abk][:, :, 0:n], r=[('ab', abk)], w=[('actT', 'all')])

        linear_fm(C, 'f1', W1, DM, slabs, C.d['xnT'], consume, tiles, after_tile, nps=4)
    T.barrier()
    with ExitStack() as st:
        xr = [alloc(nc, st, 'xr%d' % k, [128, 512], F32) for k in range(3)]
        xo = [alloc(nc, st, 'xo%d' % k, [128, 512], F32) for k in range(3)]
        xTv = C.d['xT'].rearrange("(c p) t -> p c t", p=128)
        state = {'n': 0}
        slabs = [[(s * 256, 256)] for s in range(DM // 256)]

        def consume2(si, hi, ti, tile, ps, pskey):
            t0, n, w_ = tile
            c = si * 2 + hi
            k = state['n'] % 3
            state['n'] += 1
            G = gcol(C, l, i, w_)
            T.dma('act', xr[k][:, 0:n], xTv[:, c, t0:t0 + n], r=[('xT', c, t0)], w=[('xr', k)])
            T.stt('dve', xo[k][:, 0:n], ps, G[:, c:c + 1], xr[k][:, 0:n], ALU.mult, ALU.add,
                  r=[pskey, 'Gcol', ('xr', k)], w=[('xo', k)])
            T.dma('sp', xTv[:, c, t0:t0 + n], xo[k][:, 0:n], r=[('xo', k)], w=[('xT', c, t0)])

        linear_fm(C, 'f2', W2, HFF, slabs, C.d['actT'], consume2, tiles, None, nps=4, wcols=256)
    T.barrier()


PX_WCOL = [0, 1024, 2048, 3072, 4096, 5120, 6144, 7200, 8224, 9248, 10272, 11296, 12320, 13344, 14368]
(B_AQ, B_AK, B_AV, B_GQ, B_GK, B_GV, B_GZ, B_HQ, B_HF0, B_HF1, B_HI, B_HOG, B_YV, B_YX0, B_YX1) = range(15)
PX_SMALL = 15 * 1024
PXR = 15 * 1024 + 32


def phase_proj(C, l, tiles=None):
    nc, T = C.nc, C.T
    tiles = tiles if tiles is not None else token_tiles()
    phase_norm(C, l, 1, tiles)
    T.barrier()
    with ExitStack() as st:
        stg = [alloc(nc, st, 'pst%d' % k, [128, 4, 512], F32) for k in range(2)]
        slabs = []
        dest = []
        for bi, wc in enumerate(PX_WCOL):
            for hh in range(2):
                slabs.append([(wc + hh * 512, 512)])
                dest.append(bi * 1024 + hh * 512)
        slabs.append([(7168, 32)])
        dest.append(PX_SMALL)
        state = {'n': 0}

        def consume(si, hi, ti, tile, ps, pskey):
            t0, n, w_ = tile
            k = (si * len(tiles) + ti) % 2
            cw = 32 if si == len(slabs) - 1 else 128
            eng = 'act' if state['n'] % 2 == 0 else 'dve'
            state['n'] += 1
            T.copy(eng, stg[k][0:cw, hi, 0:n], ps, r=[pskey], w=[('pst', k)])

        def after_tile(si, ti, tile):
            t0, n, w_ = tile
            k = (si * len(tiles) + ti) % 2
            if si == len(slabs) - 1:
                T.dma('sp', C.d['pxT'][PX_SMALL:PX_SMALL + 32, t0:t0 + n], stg[k][0:32, 0, 0:n],
                      r=[('pst', k)], w=[('pxT', 'all')])
            else:
                dv = C.d['pxT'][dest[si]:dest[si] + 512, t0:t0 + n].rearrange("(c p) t -> p c t", p=128)
                T.dma('sp', dv, stg[k][:, :, 0:n], r=[('pst', k)], w=[('pxT', 'all')])

        linear_fm(C, 'pj', C.W['w_in'][l], DM, slabs, C.d['xnT'], consume, tiles, after_tile, nps=4)
    T.barrier()


def phase_attn(C, l, need_ctx):
    nc, T = C.nc, C.T
    T.barrier()
    lam_init = 0.8 - 0.6 * math.exp(-0.3 * l)
    NKC = NTOK // 128
    with ExitStack() as st:
        blk1 = alloc(nc, st, 'blk1', [128, 128], F32)
        rotm = alloc(nc, st, 'rotm', [128, 128], F32)
        cosf = alloc(nc, st, 'cosf', [128, NLAT], F32)
        sinf = alloc(nc, st, 'sinf', [128, NLAT], F32)
        gqk = alloc(nc, st, 'gqk', [128, 2], F32)
        gsub = alloc(nc, st, 'gsub', [128, 1], F32)
        lamb = alloc(nc, st, 'lamb', [128, 256], F32)
        lt = alloc(nc, st, 'lt', [128, 64], F32)
        lsc = alloc(nc, st, 'lsc', [128, 4], F32)
        xq = alloc(nc, st, 'xq', [128, NTOK], F32)
        xk = alloc(nc, st, 'xk', [128, NTOK], F32)
        xv = alloc(nc, st, 'xv', [128, NTOK], F32)
        qr = alloc(nc, st, 'qr', [128, NTOK], BF16)
        kr = alloc(nc, st, 'kr', [128, NTOK], BF16)
        vaug = alloc(nc, st, 'vaug', [128, NKC, 130], BF16)
        sqt = [alloc(nc, st, 'sqt%d' % k, [128, 512], F32) for k in range(2)]
        rst = [alloc(nc, st, 'rst%d' % k, [128, 512], F32) for k in range(2)]
        xnt = [alloc(nc, st, 'xnt%d' % k, [128, 512], F32) for k in range(2)]
        ta = [alloc(nc, st, 'ta%d' % k, [128, 512], F32) for k in range(2)]
        tb = [alloc(nc, st, 'tb%d' % k, [128, 512], F32) for k in range(2)]
        eT = [alloc(nc, st, 'eT%d' % k, [128, 512], BF16) for k in range(4)]
        dd = [alloc(nc, st, 'dd%d' % k, [128, 128], F32) for k in range(2)]
        junk = alloc(nc, st, 'junk', [128, 128], F32)
        sm = [alloc(nc, st, 'sm%d' % k, [128, 8], F32) for k in range(2)]
        oh = [alloc(nc, st, 'oh%d' % k, [128, 512], BF16) for k in range(2)]
        psA = [palloc(nc, st, 'psA%d' % k, [128, 512]) for k in range(2)]
        accs = [palloc(nc, st, 'acc0', [128, 3, 129]), palloc(nc, st, 'acc1', [128, 3, 129]),
                palloc(nc, st, 'acc2', [128, 2, 129])]
        ptr = palloc(nc, st, 'ptr', [128, 128])

        def acc_ap(i):
            return accs[i // 3][:, i % 3, :]

        T.dma('sp', blk1[:], C.d['blk1'], w=['blk1'])
        T.dma('sp', rotm[:], C.d['rotm'], w=['rotm'])
        T.dma('sp', cosf[:], C.d['cosf'], w=['cosf'])
        T.dma('sp', sinf[:], C.d['sinf'], w=['sinf'])
        T.dma('sp', gqk[:], C.d['attn_qk_g'][l], w=['gqk'])
        T.dma('sp', gsub[:], C.d['attn_subln_g'][l], w=['gsub'])
        T.dma('sp', lamb[:], C.d['attn_lambda'][l].partition_broadcast(128), w=['lamb'])
        T.memset('dve', vaug[:], 1.0, w=['vaug'])
        for j in range(2):
            T.tt('dve', lt[:], lamb[:, j * 128:j * 128 + 64], lamb[:, j * 128 + 64:j * 128 + 128], ALU.mult,
                 r=['lamb'], w=['lt'])
            T.op('dve', lambda e, j=j: e.reduce_sum(lsc[:, j:j + 1], lt[:], axis=AX.X), r=['lt'], w=['lsc'])
        T.act(lsc[:, 0:2], lsc[:, 0:2], AF.Exp, r=['lsc'], w=['lsc'])
        T.tt('dve', lsc[:, 2:3], lsc[:, 0:1], lsc[:, 1:2], ALU.subtract, r=['lsc'], w=['lsc'])
        T.ts('dve', lsc[:, 3:4], lsc[:, 2:3], -1.0, -lam_init, ALU.mult, ALU.add, r=['lsc'], w=['lsc'])
        neglam = lsc[:, 3:4]

        tiles = token_tiles()
        nk = 0
        for h in range(8):
            for (buf, blkid, key) in ((xq, B_AQ, 'xq'), (xk, B_AK, 'xk'), (xv, B_AV, 'xv')):
                r0 = blkid * 1024 + h * 128
                T.dma('sp', buf[:], C.d['pxT'][r0:r0 + 128, :], r=[('pxT', 'all')], w=[key])
            for (src, skey, dst, dkey, gi) in ((xq, 'xq', qr, 'qr', 0), (xk, 'xk', kr, 'kr', 1)):
                for (t0, n, w_) in tiles:
                    if gi == 0 and w_ == 1 and not need_ctx:
                        continue
                    k = nk % 2
                    nk += 1
                    T.act(sqt[k][:, 0:n], src[:, t0:t0 + n], AF.Square, r=[skey], w=[('sqt', k)])
                    T.matmul(psA[k][:, 0:n], blk1[:], sqt[k][:, 0:n], r=['blk1', ('sqt', k)], w=[('psA', k)])
                    T.ts('dve', rst[k][:, 0:n], psA[k][:, 0:n], 1.0 / 64, EPS, ALU.mult, ALU.add, r=[('psA', k)], w=[('rst', k)])
                    T.act(rst[k][:, 0:n], rst[k][:, 0:n], AF.Sqrt, r=[('rst', k)], w=[('rst', k)])
                    T.recip(rst[k][:, 0:n], rst[k][:, 0:n], r=[('rst', k)], w=[('rst', k)])
                    if w_ == 0:
                        T.stt('dve', xnt[k][:, 0:n], src[:, t0:t0 + n], gqk[:, gi:gi + 1], rst[k][:, 0:n], ALU.mult, ALU.mult,
                              r=[skey, 'gqk', ('rst', k)], w=[('xnt', k)])
                        T.matmul(psA[k][:, 0:n], rotm[:], xnt[k][:, 0:n], r=['rotm', ('xnt', k)], w=[('psA', k)])
                        T.tt('pool', ta[k][:, 0:n], xnt[k][:, 0:n], cosf[:, t0:t0 + n], ALU.mult, r=[('xnt', k), 'cosf'], w=[('ta', k)])
                        T.tt('dve', tb[k][:, 0:n], psA[k][:, 0:n], sinf[:, t0:t0 + n], ALU.mult, r=[('psA', k), 'sinf'], w=[('tb', k)])
                        T.tt('pool', dst[:, t0:t0 + n], ta[k][:, 0:n], tb[k][:, 0:n], ALU.add, r=[('ta', k), ('tb', k)], w=[dkey])
                    else:
                        T.stt('dve', dst[:, t0:t0 + n], src[:, t0:t0 + n], gqk[:, gi:gi + 1], rst[k][:, 0:n], ALU.mult, ALU.mult,
                              r=[skey, 'gqk', ('rst', k)], w=[dkey])
            for kc in range(NKC):
                T.transpose(ptr[:], xv[:, kc * 128:(kc + 1) * 128], C.ident[:], r=['xv', 'ident'], w=['ptr'])
                T.copy('act' if kc % 2 else 'dve', vaug[:, kc, 0:128], ptr[:], r=['ptr'], w=['vaug'])
            qblocks = [(q0, 512, list(range(NKC))) for q0 in range(0, NLAT, 512)]
            if need_ctx:
                qblocks.append((NLAT, NCTX, list(range(NLAT // 128, NKC))))
            ne = 0
            for bi, (q0, qn, kcs) in enumerate(qblocks):
                nqs = qn // 128
                started = set()
                for ki, kc in enumerate(kcs):
                    for comp in range(2):
                        k = ne % 2
                        e_ = ne % 4
                        ne += 1
                        T.matmul(psA[k][:, 0:qn], kr[comp * 64:(comp + 1) * 64, kc * 128:(kc + 1) * 128],
                                 qr[comp * 64:(comp + 1) * 64, q0:q0 + qn], r=['kr', 'qr'], w=[('psA', k)])
                        T.act(eT[e_][:, 0:qn], psA[k][:, 0:qn], AF.Exp, scale=0.125, r=[('psA', k)], w=[('eT', e_)])
                        for qs in range(nqs):
                            ai = comp * 4 + qs
                            st_ = (ki == 0 and (ai // 3) not in started)
                            if ki == 0:
                                started.add(ai // 3)
                            T.mm_raw(acc_ap(ai), eT[e_][:, qs * 128:(qs + 1) * 128], vaug[:, kc, 0:129],
                                     start=st_, stop=(ki == len(kcs) - 1),
                                     r=[('eT', e_), 'vaug'], w=[('acc', ai)], sig=(qs == nqs - 1))
                ob = bi % 2
                for qs in range(nqs):
                    O0 = acc_ap(qs)
                    O1 = acc_ap(4 + qs)
                    k = qs % 2
                    s_ = sm[k]
                    T.recip(s_[:, 0:1], O0[:, 128:129], r=[('acc', qs)], w=[('sm', k)])
                    T.recip(s_[:, 1:2], O1[:, 128:129], r=[('acc', 4 + qs)], w=[('sm', k)])
                    T.tt('dve', s_[:, 2:3], s_[:, 1:2], neglam, ALU.mult, r=[('sm', k), 'lsc'], w=[('sm', k)])
                    T.ts('dve', dd[k][:], O0[:, 0:128], s_[:, 0:1], None, ALU.mult, r=[('acc', qs), ('sm', k)], w=[('dd', k)])
                    T.stt('dve', dd[k][:], O1[:, 0:128], s_[:, 2:3], dd[k][:], ALU.mult, ALU.add,
                          r=[('acc', 4 + qs), ('sm', k), ('dd', k)], w=[('dd', k)])
                    T.act(junk[:], dd[k][:], AF.Square, accum_out=s_[:, 3:4], r=[('dd', k)], w=['junk', ('sm', k)])
                    T.ts('dve', s_[:, 4:5], s_[:, 3:4], 1.0 / 128, EPS, ALU.mult, ALU.add, r=[('sm', k)], w=[('sm', k)])
                    T.act(s_[:, 4:5], s_[:, 4:5], AF.Sqrt, r=[('sm', k)], w=[('sm', k)])
                    T.recip(s_[:, 5:6], s_[:, 4:5], r=[('sm', k)], w=[('sm', k)])
                    T.ts('dve', dd[k][:], dd[k][:], s_[:, 5:6], 1.0 - lam_init, ALU.mult, ALU.mult, r=[('dd', k), ('sm', k)], w=[('dd', k)])
                    T.transpose(ptr[:], dd[k][:], C.ident[:], r=[('dd', k), 'ident'], w=['ptr'])
                    T.ts('dve', oh[ob][:, qs * 128:(qs + 1) * 128], ptr[:], gsub[:, 0:1], None, ALU.mult,
                         r=['ptr', 'gsub'], w=[('oh', ob)])
                T.dma('sp', C.d['oT'][h * 128:(h + 1) * 128, q0:q0 + qn], oh[ob][:, 0:qn], r=[('oh', ob)], w=[('oT', 'all')])
        T.flush()


def phase_merge(C, l, tiles=None):
    nc, T = C.nc, C.T
    tiles = tiles if tiles is not None else token_tiles()
    sgv = C.d['sgT'].rearrange("(c p) t -> p c t", p=128)
    acv = C.d['acc32'].rearrange("(c p) t -> p c t", p=128)
    abv = C.d['accT'].rearrange("(c p) t -> p c t", p=128)
    for i in range(4):
        T.barrier()
        with ExitStack() as st:
            sgb = [alloc(nc, st, 'sgb%d' % k, [128, 4, 512], BF16) for k in range(2)]
            slabs = [[(s * 512, 512)] for s in range(DM // 512)]

            def consumeA(si, hi, ti, tile, ps, pskey):
                t0, n, w_ = tile
                k = (si * len(tiles) + ti) % 2
                T.act(sgb[k][:, hi, 0:n], ps, AF.Sigmoid, r=[pskey], w=[('sgb', k)])

            def afterA(si, ti, tile):
                t0, n, w_ = tile
                k = (si * len(tiles) + ti) % 2
                T.dma('sp', sgv[:, si * 4:si * 4 + 4, t0:t0 + n], sgb[k][:, :, 0:n], r=[('sgb', k)], w=[('sgT', 'all')])

            linear_fm(C, 'mg', C.W['w_gate'][l * 4 + i], DM, slabs, C.d['xnT'], consumeA, tiles, afterA, nps=4)
        T.barrier()
        with ExitStack() as st:
            sgl = [alloc(nc, st, 'sgl%d' % k, [128, 512], BF16) for k in range(3)]
            acl = [alloc(nc, st, 'acl%d' % k, [128, 512], F32) for k in range(3)]
            pr = [alloc(nc, st, 'pr%d' % k, [128, 512], F32) for k in range(3)]
            prb = [alloc(nc, st, 'prb%d' % k, [128, 512], BF16) for k in range(3)]
            slabs = [[(s * 512, 512)] for s in range(DM // 512)]
            state = {'n': 0}

            def consumeB(si, hi, ti, tile, ps, pskey, i=i):
                t0, n, w_ = tile
                c = si * 4 + hi
                k = state['n'] % 3
                state['n'] += 1
                T.dma('act', sgl[k][:, 0:n], sgv[:, c, t0:t0 + n], r=[('sgT', 'all')], w=[('sgl', k)])
                if i > 0:
                    T.dma('act', acl[k][:, 0:n], acv[:, c, t0:t0 + n], r=[('acc32', c, t0)], w=[('acl', k)])
                T.tt('dve', pr[k][:, 0:n], ps, sgl[k][:, 0:n], ALU.mult, r=[pskey, ('sgl', k)], w=[('pr', k)])
                if i == 0:
                    T.dma('sp', acv[:, c, t0:t0 + n], pr[k][:, 0:n], r=[('pr', k)], w=[('acc32', c, t0)])
                elif i < 3:
                    T.tt('pool', pr[k][:, 0:n], pr[k][:, 0:n], acl[k][:, 0:n], ALU.add, r=[('pr', k), ('acl', k)], w=[('pr', k)])
                    T.dma('sp', acv[:, c, t0:t0 + n], pr[k][:, 0:n], r=[('pr', k)], w=[('acc32', c, t0)])
                else:
                    T.tt('pool', prb[k][:, 0:n], pr[k][:, 0:n], acl[k][:, 0:n], ALU.add, r=[('pr', k), ('acl', k)], w=[('prb', k)])
                    T.dma('sp', abv[:, c, t0:t0 + n], prb[k][:, 0:n], r=[('prb', k)], w=[('accT', 'all')])

            linear_fm(C, 'mu', C.W['w_up'][l * 4 + i], 1024, slabs, C.d['oT'][i * 1024:(i + 1) * 1024, :], consumeB,
                      tiles, None, nps=4)
    T.barrier()
    with ExitStack() as st:
        xr = [alloc(nc, st, 'mxr%d' % k, [128, 512], F32) for k in range(3)]
        xo = [alloc(nc, st, 'mxo%d' % k, [128, 512], F32) for k in range(3)]
        xTv = C.d['xT'].rearrange("(c p) t -> p c t", p=128)
        state = {'n': 0}
        slabs = [[(s * 512, 512)] for s in range(DM // 512)]

        def consumeO(si, hi, ti, tile, ps, pskey):
            t0, n, w_ = tile
            c = si * 4 + hi
            k = state['n'] % 3
            state['n'] += 1
            G = gcol(C, l, 1, w_)
            T.dma('act', xr[k][:, 0:n], xTv[:, c, t0:t0 + n], r=[('xT', c, t0)], w=[('mxr', k)])
            T.stt('dve', xo[k][:, 0:n], ps, G[:, c:c + 1], xr[k][:, 0:n], ALU.mult, ALU.add,
                  r=[pskey, 'Gcol', ('mxr', k)], w=[('mxo', k)])
            T.dma('sp', xTv[:, c, t0:t0 + n], xo[k][:, 0:n], r=[('mxo', k)], w=[('xT', c, t0)])

        linear_fm(C, 'mo', C.W['w_out'][l], DM, slabs, C.d['accT'], consumeO, tiles, None, nps=4)
    T.barrier()


def gated_head_norm(C, st, oacc, zbuf, gn_col, row0, need_ctx, pss, tg, psskey):
    nc, T = C.nc, C.T
    sq = [alloc(nc, st, tg + 'nsq%d' % k, [128, 512], F32) for k in range(2)]
    rs = [alloc(nc, st, tg + 'nrs%d' % k, [128, 512], F32) for k in range(2)]
    ob = [alloc(nc, st, tg + 'nob%d' % k, [128, 512], BF16) for k in range(2)]

    def run(okey, zkey, row0):
        for ti, (t0, n, w_) in enumerate(token_tiles()):
            if w_ == 1 and not need_ctx:
                continue
            k = ti % 2
            T.act(sq[k][:, 0:n], oacc[:, t0:t0 + n], AF.Square, r=[okey], w=[(tg + 'nsq', k)])
            T.matmul(pss[:, 0:n], C.ones_f[:], sq[k][:, 0:n], r=['ones_f', (tg + 'nsq', k)], w=[psskey])
            T.ts('dve', rs[k][:, 0:n], pss[:, 0:n], 1.0 / 128, EPS, ALU.mult, ALU.add, r=[psskey], w=[(tg + 'nrs', k)])
            T.act(rs[k][:, 0:n], rs[k][:, 0:n], AF.Sqrt, r=[(tg + 'nrs', k)], w=[(tg + 'nrs', k)])
            T.recip(rs[k][:, 0:n], rs[k][:, 0:n], r=[(tg + 'nrs', k)], w=[(tg + 'nrs', k)])
            T.stt('dve', rs[k][:, 0:n], oacc[:, t0:t0 + n], gn_col, rs[k][:, 0:n], ALU.mult, ALU.mult,
                  r=[okey, (tg + 'nrs', k)], w=[(tg + 'nrs', k)])
            T.act(sq[k][:, 0:n], zbuf[:, t0:t0 + n], AF.Silu, r=[zkey], w=[(tg + 'nsq', k)])
            T.tt('pool', ob[k][:, 0:n], rs[k][:, 0:n], sq[k][:, 0:n], ALU.mult, r=[(tg + 'nrs', k), (tg + 'nsq', k)], w=[(tg + 'nob', k)])
            T.dma('sp', C.d['oT'][row0:row0 + 128, t0:t0 + n], ob[k][:, 0:n], r=[(tg + 'nob', k)], w=[('oT', 'all')])
    return run


def chunk_order(d, need_first_ctx=True):
    lat = list(range(NLAT // 64))
    ctx = list(range(NLAT // 64, NTOK // 64))
    if d == 1:
        lat = lat[::-1]
        ctx = ctx[::-1]
    return ctx + lat


def phase_hgrn(C, l, need_ctx):
    nc, T = C.nc, C.T
    T.barrier()
    NKC = NTOK // 128
    NCH = NTOK // 64
    with ExitStack() as st:
        lbc = alloc(nc, st, 'lbc', [128, 16], F32)
        oml = alloc(nc, st, 'oml', [128, 16], F32)
        noml = alloc(nc, st, 'noml', [128, 16], F32)
        lbf = alloc(nc, st, 'lbf', [128, 16], F32)
        lgr = alloc(nc, st, 'lgr', [16, 2, 128], F32)
        gn = alloc(nc, st, 'hgn', [128, 1], F32)
        msk = alloc(nc, st, 'hmsk', [128, 2, 64], F32)
        rmask = alloc(nc, st, 'rmask', [128, NTOK], F32)
        xq = alloc(nc, st, 'hxq', [128, NTOK], F32)
        oacc = alloc(nc, st, 'hoacc', [128, NTOK], F32)
        vtm = alloc(nc, st, 'hvtm', [128, NKC, 128], F32)
        ktm = alloc(nc, st, 'hktm', [128, NKC, 128], F32)
        B = [alloc(nc, st, 'hB%d' % k, [128, NTOK], F32) for k in range(7)]
        KF = [alloc(nc, st, 'hKF%d' % k, [128, NTOK], F32) for k in range(4)]
        dl = alloc(nc, st, 'hdl', [128, NCH], F32)
        S = alloc(nc, st, 'hS', [128, 128], F32)
        aTm = [alloc(nc, st, 'haTm%d' % k, [128, 64], F32) for k in range(2)]
        ptr = palloc(nc, st, 'hptr', [128, 512])
        paT = [palloc(nc, st, 'hpaT%d' % k, [128, 64]) for k in range(2)]
        poT = [palloc(nc, st, 'hpoT%d' % k, [128, 64]) for k in range(2)]
        pkv = [palloc(nc, st, 'hpkv%d' % k, [128, 128]) for k in range(2)]
        ghn = gated_head_norm(C, st, oacc, B[0], gn[:, 0:1], 0, need_ctx, ptr, 'h', 'hptr')

        T.dma('sp', msk[:], C.d['hmask'], w=['hmsk'])
        T.dma('sp', gn[:], C.d['hg_norm_g'][l], w=['hgn'])
        T.memset('dve', rmask[:], 1.0, w=['rmask'])
        T.memset('dve', rmask[:].rearrange("p (c k) -> p c k", k=64)[:, :, 0:1], 0.0, w=['rmask'])
        if l == 0:
            T.memset('dve', lbc[:], 0.0, w=['lbc'])
        else:
            T.dma('sp', lgr[:], C.d['hg_lb_logits'], w=['lgr'])
            T.tt('dve', lgr[:, 0, :], lgr[:, 1, :], lgr[:, 0, :], ALU.subtract, r=['lgr'], w=['lgr'])
            T.transpose(ptr[:, 0:16], lgr[:, 0, :], C.ident[0:16, 0:16], r=['lgr', 'ident'], w=['hptr'])
            T.act(lbc[:], ptr[:, 0:16], AF.Sigmoid, r=['hptr'], w=['lbc'])
        T.ts('dve', oml[:], lbc[:], -1.0, 1.0, ALU.mult, ALU.add, r=['lbc'], w=['oml'])
        T.ts('dve', noml[:], oml[:], -1.0, None, ALU.mult, r=['oml'], w=['noml'])
        T.ts('dve', lbf[:], lbc[:], 1e-20, None, ALU.max, r=['lbc'], w=['lbf'])

        def v3(buf):
            return buf[:].rearrange("p (c k) -> p c k", k=64)

        npp = 0
        for h in range(8):
            r0 = h * 128
            T.dma('sp', xq[:], C.d['pxT'][B_HQ * 1024 + r0:B_HQ * 1024 + r0 + 128, :], r=[('pxT', 'all')], w=['hxq'])
            T.dma('sp', B[0][:], C.d['pxT'][B_HI * 1024 + r0:B_HI * 1024 + r0 + 128, :], r=[('pxT', 'all')], w=[('hB', 0)])
            for kc in range(NKC):
                T.transpose(ptr[:, 0:128], B[0][:, kc * 128:(kc + 1) * 128], C.ident[:], r=[('hB', 0), 'ident'], w=['hptr'])
                T.copy('act' if kc % 2 else 'dve', vtm[:, kc, :], ptr[:, 0:128], r=['hptr'], w=['hvtm'])
            for d in range(2):
                col = d * 8 + h
                fr0 = (B_HF0 + d) * 1024 + r0
                T.dma('sp', B[0][:], C.d['pxT'][fr0:fr0 + 128, :], r=[('pxT', 'all')], w=[('hB', 0)])
                T.act(B[0][:], B[0][:], AF.Sigmoid, r=[('hB', 0)], w=[('hB', 0)])
                T.act(B[1][:], B[0][:], AF.Ln, bias=lbf[:, col:col + 1], scale=oml[:, col:col + 1],
                      r=[('hB', 0), 'lbf', 'oml'], w=[('hB', 1)])
                T.ts('dve', B[2][:], B[0][:], noml[:, col:col + 1], oml[:, col:col + 1], ALU.mult, ALU.add,
                     r=[('hB', 0), 'noml', 'oml'], w=[('hB', 2)])
                T.op('dve', lambda e: e.tensor_tensor_scan(B[0][:], rmask[:], B[1][:], 0.0, ALU.mult, ALU.add),
                     r=['rmask', ('hB', 1)], w=[('hB', 0)])
                if d == 0:
                    gcb, gck = B[0], ('hB', 0)
                    last_k, = (63,)
                else:
                    T.tt('dve', B[3][:], B[1][:], B[0][:], ALU.subtract, r=[('hB', 1), ('hB', 0)], w=[('hB', 3)])
                    T.tt('dve', v3(B[3]), v3(B[3]), v3(B[0])[:, :, 63:64].to_broadcast([128, NCH, 64]), ALU.add,
                         r=[('hB', 3), ('hB', 0)], w=[('hB', 3)])
                    gcb, gck = B[3], ('hB', 3)
                    last_k = 0
                gc3 = v3(gcb)
                off = 0 if d == 0 else 15
                g16 = gcb[:].rearrange("p (c k) -> p c k", k=16)
                T.tt('dve', B[1][:].rearrange("p (c k) -> p c k", k=16), g16, g16[:, :, off:off + 1].to_broadcast([128, NTOK // 16, 16]),
                     ALU.subtract, r=[gck], w=[('hB', 1)])
                T.act(B[4][:], B[1][:], AF.Exp, r=[('hB', 1)], w=[('hB', 4)])
                T.tt('pool', B[4][:], B[4][:], xq[:], ALU.mult, r=[('hB', 4), 'hxq'], w=[('hB', 4)])
                for I in range(4):
                    ko = 16 * I + off
                    T.tt('dve', v3(KF[I]), gc3[:, :, ko:ko + 1].to_broadcast([128, NCH, 64]), gc3, ALU.subtract,
                         r=[gck], w=[('hKF', I)])
                    T.ts('pool', KF[I][:], KF[I][:], 60.0, None, ALU.min, r=[('hKF', I)], w=[('hKF', I)])
                    T.act(KF[I][:], KF[I][:], AF.Exp, r=[('hKF', I)], w=[('hKF', I)])
                    T.tt('pool', KF[I][:], KF[I][:], B[2][:], ALU.mult, r=[('hKF', I), ('hB', 2)], w=[('hKF', I)])
                T.act(B[5][:], gcb[:], AF.Exp, r=[gck], w=[('hB', 5)])
                T.tt('dve', B[5][:], B[5][:], xq[:], ALU.mult, r=[('hB', 5), 'hxq'], w=[('hB', 5)])
                T.tt('dve', v3(B[6]), gc3[:, :, last_k:last_k + 1].to_broadcast([128, NCH, 64]), gc3, ALU.subtract,
                     r=[gck], w=[('hB', 6)])
                T.act(B[6][:], B[6][:], AF.Exp, r=[('hB', 6)], w=[('hB', 6)])
                T.tt('pool', B[6][:], B[6][:], B[2][:], ALU.mult, r=[('hB', 6), ('hB', 2)], w=[('hB', 6)])
                T.act(dl[:], gc3[:, :, last_k], AF.Exp, r=[gck], w=['hdl'])
                for kc in range(NKC):
                    T.transpose(ptr[:, 0:128], B[6][:, kc * 128:(kc + 1) * 128], C.ident[:], r=[('hB', 6), 'ident'], w=['hptr'])
                    T.copy('act' if kc % 2 else 'dve', ktm[:, kc, :], ptr[:, 0:128], r=['hptr'], w=['hktm'])
                if getattr(C, 'dbg', None) == (h, d):
                    for bi_ in range(7):
                        T.dma('sp', C.d['dbg'][bi_], B[bi_][:], r=[('hB', bi_)], w=[('dbg', bi_)])
                T.memset('dve', S[:], 0.0, w=['hS'])
                for c in chunk_order(d):
                    t0 = c * 64
                    pb = t0 % 128
                    kc = t0 // 128
                    is_ctx = c >= NLAT // 64
                    p = npp % 2
                    npp += 1
                    if not (is_ctx and not need_ctx):
                        for I in range(4):
                            T.matmul(paT[p][pb:pb + 64, 16 * I:16 * I + 16], KF[I][:, t0:t0 + 64],
                                     B[4][:, t0 + 16 * I:t0 + 16 * I + 16],
                                     r=[('hKF', I), ('hB', 4)], w=[('hpaT', p)], sig=(I == 3))
                        T.stt('dve', aTm[p][pb:pb + 64, :], paT[p][pb:pb + 64, :], 3.0e38, msk[pb:pb + 64, d, :], ALU.min, ALU.mult,
                              r=[('hpaT', p), 'hmsk'], w=[('haTm', p)])
                        T.matmul(poT[p][:, :], S[:], B[5][:, t0:t0 + 64], start=True, stop=False,
                                 r=['hS', ('hB', 5)], w=[('hpoT', p)], sig=False)
                        T.matmul(poT[p][:, :], vtm[pb:pb + 64, kc, :], aTm[p][pb:pb + 64, :], start=False, stop=True,
                                 r=['hvtm', ('haTm', p)], w=[('hpoT', p)])
                        if d == 0:
                            T.copy('act', oacc[:, t0:t0 + 64], poT[p][:, :], r=[('hpoT', p)], w=['hoacc'])
                        else:
                            T.tt('pool' if False else 'dve', oacc[:, t0:t0 + 64], oacc[:, t0:t0 + 64], poT[p][:, :], ALU.add,
                                 r=['hoacc', ('hpoT', p)], w=['hoacc'])
                    T.matmul(pkv[p][:], ktm[pb:pb + 64, kc, :], vtm[pb:pb + 64, kc, :], r=['hktm', 'hvtm'], w=[('hpkv', p)])
                    T.stt('dve', S[:], S[:], dl[:, c:c + 1], pkv[p][:], ALU.mult, ALU.add, r=['hS', 'hdl', ('hpkv', p)], w=['hS'])
            T.dma('sp', B[0][:], C.d['pxT'][B_HOG * 1024 + r0:B_HOG * 1024 + r0 + 128, :], r=[('pxT', 'all')], w=[('hB', 0)])
            ghn('hoacc', ('hB', 0), 2 * 1024 + r0)
        T.flush()


def phase_gdn(C, l, need_ctx):
    nc, T = C.nc, C.T
    T.barrier()
    NKC = NTOK // 128
    NCH = NTOK // 64
    segs = [(0, NLAT), (NLAT, NTOK)]
    with ExitStack() as st:
        sel = alloc(nc, st, 'gsel', [16, 16 * 128], F32)
        gmask = alloc(nc, st, 'gmask', [128, 4, 128], F32)
        cw = alloc(nc, st, 'gcw', [128, 72], F32)
        gn = alloc(nc, st, 'ggn', [128, 1], F32)
        Rgc = alloc(nc, st, 'Rgc', [16, NTOK], F32)
        Rnb = alloc(nc, st, 'Rnb', [16, NTOK], F32)
        cols = alloc(nc, st, 'gcols', [128, NKC, 6, 16], F32)
        dlall = alloc(nc, st, 'gdl', [128, 16, NCH], F32)
        ptr = palloc(nc, st, 'gptr', [128, 512])
        pA = [palloc(nc, st, 'gpA%d' % k, [128, 128]) for k in range(4)]
        pB = [palloc(nc, st, 'gpB%d' % k, [128, 128]) for k in range(3)]
        T.dma('sp', sel[:], C.d['sel16'], w=['gsel'])
        T.dma('sp', gmask[:], C.d['gmask'], w=['gmask'])
        T.dma('sp', cw[:], C.d['gdn_conv_w'][l], w=['gcw'])
        T.dma('sp', gn[:], C.d['gdn_norm_g'][l], w=['ggn'])
        with ExitStack() as st2:
            R = [alloc(nc, st2, 'gR%d' % k, [16, NTOK], F32) for k in range(6)]
            prm = alloc(nc, st2, 'gprm', [16, 4], F32)
            nea = alloc(nc, st2, 'gnea', [16, 1], F32)
            gcl = alloc(nc, st2, 'ggcl', [16, NCH], F32)
            gt = alloc(nc, st2, 'ggt', [16, NCH], F32)

            def r3(b):
                return b[:].rearrange("p (c k) -> p c k", k=64)
            T.dma('sp', prm[:], C.d['gdn_prm'][l], w=['gprm'])
            T.dma('sp', R[0][:], C.d['pxT'][PX_SMALL:PX_SMALL + 16, :], r=[('pxT', 'all')], w=[('gR', 0)])
            T.dma('sp', R[1][:], C.d['pxT'][PX_SMALL + 16:PX_SMALL + 32, :], r=[('pxT', 'all')], w=[('gR', 1)])
            T.act(nea[:], prm[:, 0:1], AF.Exp, r=['gprm'], w=['gnea'])
            T.ts('dve', nea[:], nea[:], -1.0, None, ALU.mult, r=['gnea'], w=['gnea'])
            T.act(R[0][:], R[0][:], AF.Exp, bias=prm[:, 1:2], r=[('gR', 0), 'gprm'], w=[('gR', 0)])
            T.act(R[0][:], R[0][:], AF.Ln, bias=1.0, r=[('gR', 0)], w=[('gR', 0)])
            T.ts('dve', R[0][:], R[0][:], nea[:, 0:1], None, ALU.mult, r=[('gR', 0), 'gnea'], w=[('gR', 0)])
            T.act(R[1][:], R[1][:], AF.Sigmoid, r=[('gR', 1)], w=[('gR', 1)])
            T.memset('dve', R[2][:], 1.0, w=[('gR', 2)])
            T.memset('dve', r3(R[2])[:, :, 0:1], 0.0, w=[('gR', 2)])
            T.op('dve', lambda e: e.tensor_tensor_scan(R[3][:], R[2][:], R[0][:], 0.0, ALU.mult, ALU.add),
                 r=[('gR', 2), ('gR', 0)], w=[('gR', 3)])
            T.tt('dve', R[4][:], R[0][:], R[3][:], ALU.subtract, r=[('gR', 0), ('gR', 3)], w=[('gR', 4)])
            T.tt('dve', r3(R[4]), r3(R[4]), r3(R[3])[:, :, 63:64].to_broadcast([16, NCH, 64]), ALU.add,
                 r=[('gR', 4), ('gR', 3)], w=[('gR', 4)])
            T.ts('dve', R[3][:], R[3][:], prm[:, 2:3], None, ALU.mult, r=[('gR', 3), 'gprm'], w=[('gR', 3)])
            T.stt('dve', Rgc[:], R[4][:], prm[:, 3:4], R[3][:], ALU.mult, ALU.add, r=[('gR', 4), ('gR', 3), 'gprm'], w=['Rgc'])
            g3 = Rgc[:].rearrange("p (c k) -> p c k", k=64)
            T.ts('dve', gcl[:], g3[:, :, 63], prm[:, 2:3], None, ALU.mult, r=['Rgc', 'gprm'], w=['ggcl'])
            T.stt('dve', gcl[:], g3[:, :, 0], prm[:, 3:4], gcl[:], ALU.mult, ALU.add, r=['Rgc', 'gprm', 'ggcl'], w=['ggcl'])
            T.ts('dve', Rnb[:], R[1][:], -1.0, None, ALU.mult, r=[('gR', 1)], w=['Rnb'])
            T.ts('dve', R[0][:], Rgc[:], -1.0, None, ALU.mult, r=['Rgc'], w=[('gR', 0)])
            T.act(R[2][:], Rgc[:], AF.Exp, r=['Rgc'], w=[('gR', 2)])
            T.tt('dve', R[2][:], R[2][:], R[1][:], ALU.mult, r=[('gR', 2), ('gR', 1)], w=[('gR', 2)])
            T.tt('dve', r3(R[3]), gcl[:].unsqueeze(2).to_broadcast([16, NCH, 64]), g3, ALU.subtract,
                 r=['ggcl', 'Rgc'], w=[('gR', 3)])
            T.act(R[3][:], R[3][:], AF.Exp, r=[('gR', 3)], w=[('gR', 3)])
            T.act(gt[:], gcl[:], AF.Exp, r=['ggcl'], w=['ggt'])
            srcs = [(Rgc, 'Rgc'), (R[0], ('gR', 0)), (Rnb, 'Rnb'), (R[1], ('gR', 1)), (R[2], ('gR', 2)), (R[3], ('gR', 3))]
            for kc in range(NKC):
                for j, (buf, key) in enumerate(srcs):
                    T.transpose(ptr[:, j * 16:(j + 1) * 16], buf[:, kc * 128:(kc + 1) * 128], C.ident[0:16, 0:16],
                                r=[key, 'ident'], w=['gptr'], sig=(j == 5))
                T.copy('dve', cols[:, kc, :, :], ptr[:, 0:96].rearrange("p (j r) -> p j r", j=6), r=['gptr'], w=['gcols'])
            for r_ in range(16):
                T.matmul(ptr[:, 0:NCH], sel[:, r_ * 128:(r_ + 1) * 128], gt[:], r=['gsel', 'ggt'], w=['gptr'])
                T.copy('act', dlall[:, r_, :], ptr[:, 0:NCH], r=['gptr'], w=['gdl'])
            T.flush()
        T.barrier()
        xq = alloc(nc, st, 'gxq', [128, NTOK], F32)
        xk = alloc(nc, st, 'gxk', [128, NTOK], F32)
        xv = alloc(nc, st, 'gxv', [128, NTOK], F32)
        raw = alloc(nc, st, 'graw', [128, NTOK], F32)
        oacc = alloc(nc, st, 'goacc', [128, NTOK], F32)
        S = alloc(nc, st, 'gS', [128, 128], F32)
        sq = [alloc(nc, st, 'gsq%d' % k, [128, 512], F32) for k in range(2)]
        names = ['E1', 'E2', 'egb', 'D1', 'D2s', 'D2i', 'D2sb', 'M', 'MT', 'PT', 'attnT', 'vb', 'kbg', 'kg', 'u', 'wT', 'qgT',
                 'M2', 'MT2', 'vnew']
        W = {n_: [alloc(nc, st, 'g' + n_ + str(k), [128, 128], F32) for k in range(2)] for n_ in names}
        ghn = gated_head_norm(C, st, oacc, raw, gn[:, 0:1], 0, need_ctx, ptr, 'g', 'gptr')
        nt = 0
        npa = 0
        npb = 0

        def PA():
            nonlocal npa
            npa += 1
            return pA[npa % 4], ('gpA', npa % 4)

        def PB():
            nonlocal npb
            npb += 1
            return pB[npb % 3], ('gpB', npb % 3)

        for h in range(8):
            r0 = h * 128
            for bi_, (blk, dst, dkey) in enumerate(((B_GQ, xq, 'gxq'), (B_GK, xk, 'gxk'), (B_GV, xv, 'gxv'))):
                T.dma('sp', raw[:], C.d['pxT'][blk * 1024 + r0:blk * 1024 + r0 + 128, :], r=[('pxT', 'all')], w=['graw'])
                cb = bi_ * 8 + h
                w0, w1, w2 = (cw[:, j * 24 + cb:j * 24 + cb + 1] for j in range(3))
                T.ts('dve', dst[:], raw[:], w1, None, ALU.mult, r=['graw', 'gcw'], w=[dkey])
                for (s0, s1) in segs:
                    T.stt('dve', dst[:, s0 + 1:s1], raw[:, s0:s1 - 1], w0, dst[:, s0 + 1:s1], ALU.mult, ALU.add,
                          r=['graw', 'gcw', dkey], w=[dkey])
                    T.stt('dve', dst[:, s0:s1 - 1], raw[:, s0 + 1:s1], w2, dst[:, s0:s1 - 1], ALU.mult, ALU.add,
                          r=['graw', 'gcw', dkey], w=[dkey])
                T.act(dst[:], dst[:], AF.Silu, r=[dkey], w=[dkey])
                if bi_ < 2:
                    for ti, (t0, n, w_) in enumerate(token_tiles()):
                        k = ti % 2
                        T.act(sq[k][:, 0:n], dst[:, t0:t0 + n], AF.Square, r=[dkey], w=[('gsq', k)])
                        T.matmul(ptr[:, 0:n], C.ones_f[:], sq[k][:, 0:n], r=['ones_f', ('gsq', k)], w=['gptr'])
                        T.ts('dve', sq[k][:, 0:n], ptr[:, 0:n], EPS, None, ALU.add, r=['gptr'], w=[('gsq', k)])
                        T.act(sq[k][:, 0:n], sq[k][:, 0:n], AF.Sqrt, r=[('gsq', k)], w=[('gsq', k)])
                        T.recip(sq[k][:, 0:n], sq[k][:, 0:n], r=[('gsq', k)], w=[('gsq', k)])
                        if bi_ == 0:
                            T.stt('dve', dst[:, t0:t0 + n], dst[:, t0:t0 + n], 128.0 ** -0.5, sq[k][:, 0:n], ALU.mult, ALU.mult,
                                  r=[dkey, ('gsq', k)], w=[dkey])
                        else:
                            T.tt('dve', dst[:, t0:t0 + n], dst[:, t0:t0 + n], sq[k][:, 0:n], ALU.mult, r=[dkey, ('gsq', k)], w=[dkey])
            for d in range(2):
                rr = d * 8 + h
                mA, mAT, mI = ((0, 1, 3) if d == 0 else (1, 0, 2))
                T.memset('dve', S[:], 0.0, w=['gS'])
                order = chunk_order(d)
                tiles_o = []
                for c in order:
                    if c // 2 not in tiles_o:
                        tiles_o.append(c // 2)
                for kc in tiles_o:
                    b = nt % 2
                    nt += 1
                    tk = slice(kc * 128, (kc + 1) * 128)
                    Wb = {n_: W[n_][b] for n_ in names}
                    K_ = {n_: ('g' + n_, b) for n_ in names}
                    c_gc, c_ngc, c_nb, c_b, c_bgc, c_kgf = (cols[:, kc, j, rr:rr + 1] for j in range(6))
                    pgc, kgc = PA()
                    T.matmul(pgc[:], sel[:, rr * 128:(rr + 1) * 128], Rgc[:, tk], r=['gsel', 'Rgc'], w=[kgc])
                    pnb, knb = PA()
                    T.matmul(pnb[:], sel[:, rr * 128:(rr + 1) * 128], Rnb[:, tk], r=['gsel', 'Rnb'], w=[knb])
                    T.ts('dve', Wb['D1'][:], pgc[:], c_gc, None, ALU.subtract, r=[kgc, 'gcols'], w=[K_['D1']])
                    T.ts('pool', Wb['E2'][:], Wb['D1'][:], 0.0, None, ALU.min, r=[K_['D1']], w=[K_['E2']])
                    T.ts('pool', Wb['E1'][:], Wb['D1'][:], 0.0, None, ALU.max, r=[K_['D1']], w=[K_['E1']])
                    T.act(Wb['E2'][:], Wb['E2'][:], AF.Exp, r=[K_['E2']], w=[K_['E2']])
                    T.act(Wb['E1'][:], Wb['E1'][:], AF.Exp, scale=-1.0, r=[K_['E1']], w=[K_['E1']])
                    T.act(Wb['egb'][:], pgc[:], AF.Exp, r=[kgc], w=[K_['egb']])
                    T.tt('dve', Wb['D1'][:], Wb['E1'][:], gmask[:, mA, :], ALU.mult, r=[K_['E1'], 'gmask'], w=[K_['D1']])
                    T.tt('pool', Wb['D2s'][:], Wb['E2'][:], gmask[:, mAT, :], ALU.mult, r=[K_['E2'], 'gmask'], w=[K_['D2s']])
                    T.tt('pool', Wb['D2i'][:], Wb['E2'][:], gmask[:, mI, :], ALU.mult, r=[K_['E2'], 'gmask'], w=[K_['D2i']])
                    T.tt('dve', Wb['D2sb'][:], Wb['D2s'][:], pnb[:], ALU.mult, r=[K_['D2s'], knb], w=[K_['D2sb']])
                    pG, kG = PA()
                    T.matmul(pG[:], xk[:, tk], xk[:, tk], r=['gxk'], w=[kG])
                    pQ, kQ = PA()
                    T.matmul(pQ[:], xk[:, tk], xq[:, tk], r=['gxk', 'gxq'], w=[kQ])
                    T.stt('dve', Wb['M'][:], pG[:], c_nb, Wb['D1'][:], ALU.mult, ALU.mult, r=[kG, 'gcols', K_['D1']], w=[K_['M']])
                    T.tt('dve', Wb['MT'][:], pG[:], Wb['D2sb'][:], ALU.mult, r=[kG, K_['D2sb']], w=[K_['MT']])
                    T.tt('pool', Wb['PT'][:], Wb['MT'][:], C.ident[:], ALU.add, r=[K_['MT'], 'ident'], w=[K_['PT']])
                    T.tt('dve', Wb['attnT'][:], pQ[:], Wb['D2i'][:], ALU.mult, r=[kQ, K_['D2i']], w=[K_['attnT']])
                    Mc, MTc, kM, kMT = Wb['M'], Wb['MT'], K_['M'], K_['MT']
                    Mn, MTn, kMn, kMTn = Wb['M2'], Wb['MT2'], K_['M2'], K_['MT2']
                    for it in range(5):
                        p1, k1 = PB()
                        T.matmul(p1[:], MTc[:], Mc[:], r=[kM, kMT], w=[k1])
                        T.copy('act', Mn[:], p1[:], r=[k1], w=[kMn])
                        if it < 4:
                            p2, k2 = PB()
                            T.matmul(p2[:], Mc[:], MTc[:], r=[kM, kMT], w=[k2])
                            T.copy('dve', MTn[:], p2[:], r=[k2], w=[kMTn])
                        p3, k3 = PB()
                        T.matmul(p3[:], Mn[:], Wb['PT'][:], r=[kMn, K_['PT']], w=[k3])
                        T.tt('dve', Wb['PT'][:], Wb['PT'][:], p3[:], ALU.add, r=[K_['PT'], k3], w=[K_['PT']])
                        Mc, Mn, kM, kMn = Mn, Mc, kMn, kM
                        MTc, MTn, kMT, kMTn = MTn, MTc, kMTn, kMT
                    pk, kk_ = PA()
                    T.transpose(pk[:], xk[:, tk], C.ident[:], r=['gxk', 'ident'], w=[kk_])
                    pv, kv_ = PA()
                    T.transpose(pv[:], xv[:, tk], C.ident[:], r=['gxv', 'ident'], w=[kv_])
                    T.ts('dve', Wb['vb'][:], pv[:], c_b, None, ALU.mult, r=[kv_, 'gcols'], w=[K_['vb']])
                    T.ts('dve', Wb['kbg'][:], pk[:], c_bgc, None, ALU.mult, r=[kk_, 'gcols'], w=[K_['kbg']])
                    T.act(Wb['kg'][:], pk[:], AF.Copy, scale=c_kgf, r=[kk_, 'gcols'], w=[K_['kg']])
                    pu, ku = PA()
                    T.matmul(pu[:], Wb['PT'][:], Wb['vb'][:], r=[K_['PT'], K_['vb']], w=[ku])
                    T.copy('act', Wb['u'][:], pu[:], r=[ku], w=[K_['u']])
                    pw, kw = PA()
                    T.matmul(pw[:], Wb['kbg'][:], Wb['PT'][:], r=[K_['PT'], K_['kbg']], w=[kw])
                    T.copy('act', Wb['wT'][:], pw[:], r=[kw], w=[K_['wT']])
                    T.tt('pool', Wb['qgT'][:], xq[:, tk], Wb['egb'][:], ALU.mult, r=['gxq', K_['egb']], w=[K_['qgT']])
                    for c in ([2 * kc, 2 * kc + 1] if d == 0 else [2 * kc + 1, 2 * kc]):
                        pb = (c % 2) * 64
                        t0 = c * 64
                        ps_ = slice(pb, pb + 64)
                        is_ctx = c >= NLAT // 64
                        p1, k1 = PB()
                        T.matmul(p1[ps_, :], Wb['wT'][:, ps_], S[:], r=[K_['wT'], 'gS'], w=[k1])
                        T.tt('dve', Wb['vnew'][ps_, :], Wb['u'][ps_, :], p1[ps_, :], ALU.subtract, r=[K_['u'], k1], w=[K_['vnew']])
                        if not (is_ctx and not need_ctx):
                            p2, k2 = PB()
                            T.matmul(p2[:, 0:64], S[:], Wb['qgT'][:, ps_], start=True, stop=False, r=['gS', K_['qgT']], w=[k2], sig=False)
                            T.matmul(p2[:, 0:64], Wb['vnew'][ps_, :], Wb['attnT'][ps_, ps_], start=False, stop=True,
                                     r=[K_['vnew'], K_['attnT']], w=[k2])
                            if d == 0:
                                T.copy('act', oacc[:, t0:t0 + 64], p2[:, 0:64], r=[k2], w=['goacc'])
                            else:
                                T.tt('dve', oacc[:, t0:t0 + 64], oacc[:, t0:t0 + 64], p2[:, 0:64], ALU.add, r=['goacc', k2], w=['goacc'])
                        p3, k3 = PB()
                        T.matmul(p3[:], Wb['kg'][ps_, :], Wb['vnew'][ps_, :], r=[K_['kg'], K_['vnew']], w=[k3])
                        T.stt('dve', S[:], S[:], dlall[:, rr, c:c + 1], p3[:], ALU.mult, ALU.add, r=['gS', 'gdl', k3], w=['gS'])
            T.dma('sp', raw[:], C.d['pxT'][B_GZ * 1024 + r0:B_GZ * 1024 + r0 + 128, :], r=[('pxT', 'all')], w=['graw'])
            ghn('goacc', 'graw', 1024 + r0)
        T.flush()


def phase_hyena(C, l, tok0, L, sfx):
    nc, T = C.nc, C.T
    T.barrier()
    NS = L // 128
    PI = math.pi
    SB = 128
    with ExitStack() as st:
        cw = alloc(nc, st, 'ycw', [128, 72], F32)
        cb_ = alloc(nc, st, 'ycb', [128, 24], F32)
        ybias = alloc(nc, st, 'ybias', [128, 8], F32)
        w1 = alloc(nc, st, 'yw1', [33, 64], F32)
        w2 = alloc(nc, st, 'yw2', [64, 64], F32)
        w3 = alloc(nc, st, 'yw3', [64, 2048], F32)
        vecs = alloc(nc, st, 'yvecs', [64, 4], F32)
        tcol = alloc(nc, st, 'ytcol', [128, NS, 2], F32)
        dbc = alloc(nc, st, 'ydbc', [128, 2048], F32)
        h2p = alloc(nc, st, 'yh2p', [64, L + 1], F32)
        hs = alloc(nc, st, 'yhs', [128, NS, 512], BF16)
        hd = alloc(nc, st, 'yhd', [128, NS, 512], BF16)
        vgt = alloc(nc, st, 'yvgt', [128, NS, 512], BF16)
        Yre = alloc(nc, st, 'yYre', [128, NS, 512], BF16)
        Yim = alloc(nc, st, 'yYim', [128, NS, 512], BF16)
        dec = [alloc(nc, st, 'ydec%d' % k, [128, 512], F32) for k in range(2)]
        hf = [alloc(nc, st, 'yhf%d' % k, [128, 512], F32) for k in range(2)]
        hb = [alloc(nc, st, 'yhb%d' % k, [128, 512], F32) for k in range(2)]
        raw = [alloc(nc, st, 'yraw0', [128, L], F32)] * 3
        cv = [alloc(nc, st, 'ycv%d' % k, [128, L], F32) for k in range(3)]
        pF = [palloc(nc, st, 'ypF%d' % k, [128, 512]) for k in range(2)]
        pK = [palloc(nc, st, 'ypK%d' % k, [128, 512]) for k in range(2)]
        pX = [palloc(nc, st, 'ypX%d' % k, [128, 512]) for k in range(2)]

        T.dma('sp', cw[:], C.d['hy_conv_w'][l], w=['ycw'])
        T.dma('sp', cb_[:], C.d['hy_conv_b'][l], w=['ycb'])
        T.dma('sp', ybias[:], C.d['hy_bias'][l], w=['ybias'])
        T.dma('sp', w1[:], C.d['hy_w1'][l], w=['yw1'])
        T.dma('sp', w2[:], C.d['hy_w2'][l], w=['yw2'])
        T.dma('sp', w3[:], C.d['hy_w3'][l], w=['yw3'])
        T.dma('sp', vecs[:], C.d['hy_vecs'][l], w=['yvecs'])
        T.dma('sp', tcol[:], C.d['hy_tcol' + sfx], w=['ytcol'])
        T.dma('sp', dbc[:], C.d['hy_delta'].partition_broadcast(128), w=['ydbc'])
        T.memset('dve', h2p[:, 0:1], 0.0, w=['yh2p'])
        st2 = ExitStack()
        zT = alloc(nc, st2, 'yzT', [33, L], F32)
        h1 = alloc(nc, st2, 'yh1', [64, L], F32)
        arg = [alloc(nc, st2, 'yarg%d' % k, [64, 512], F32) for k in range(2)]
        argi = [alloc(nc, st2, 'yargi%d' % k, [64, 512], mybir.dt.int32) for k in range(2)]
        argf = [alloc(nc, st2, 'yargf%d' % k, [64, 512], F32) for k in range(2)]
        T.dma('sp', zT[:], C.d['hy_zT' + sfx], w=['yzT'])
        for stage in range(2):
            for b0 in range(0, L, 512):
                n = min(512, L - b0)
                k = (b0 // 512) % 2
                if stage == 0:
                    T.matmul(pF[k][0:64, 0:n], w1[:], zT[:, b0:b0 + n], r=['yw1', 'yzT'], w=[('ypF', k)])
                else:
                    T.matmul(pF[k][0:64, 0:n], w2[:], h1[:, b0:b0 + n], r=['yw2', 'yh1'], w=[('ypF', k)])
                T.ts('dve', arg[k][:, 0:n], pF[k][0:64, 0:n], vecs[:, stage:stage + 1], vecs[:, 2 + stage:3 + stage],
                     ALU.add, ALU.mult, r=[('ypF', k), 'yvecs'], w=[('yarg', k)])
                T.ts('dve', argi[k][:, 0:n], arg[k][:, 0:n], 1.0 / (2.0 * PI), None, ALU.mult, r=[('yarg', k)], w=[('yargi', k)])
                T.copy('dve', argf[k][:, 0:n], argi[k][:, 0:n], r=[('yargi', k)], w=[('yargf', k)])
                T.stt('dve', arg[k][:, 0:n], argf[k][:, 0:n], -2.0 * PI, arg[k][:, 0:n], ALU.mult, ALU.add,
                      r=[('yargf', k), ('yarg', k)], w=[('yarg', k)])
                if stage == 0:
                    T.act(h1[:, b0:b0 + n], arg[k][:, 0:n], AF.Sin, r=[('yarg', k)], w=['yh1'])
                else:
                    T.act(h2p[:, 1 + b0:1 + b0 + n], arg[k][:, 0:n], AF.Sin, r=[('yarg', k)], w=['yh2p'])
        T.flush()
        st2.close()
        T.barrier()
        cms = [alloc(nc, st, 'ycm%d' % k, [128, NS, 128], BF16) for k in range(4)]
        cts = [alloc(nc, st, 'yct%d' % k, [128, NS, SB], BF16) for k in range(4)]
        kk = [alloc(nc, st, 'ykk%d' % k, [128, 512], F32) for k in range(2)]
        t1 = [alloc(nc, st, 'yt1%d' % k, [128, 512], F32) for k in range(2)]
        t2 = [alloc(nc, st, 'yt2%d' % k, [128, 512], F32) for k in range(2)]
        ev = [alloc(nc, st, 'yev%d' % k, [128, SB], F32) for k in range(2)]
        ex = [alloc(nc, st, 'yex%d' % k, [128, SB], F32) for k in range(2)]
        eo = [alloc(nc, st, 'yeo%d' % k, [128, SB], BF16) for k in range(2)]
        ncm = 0
        nct = 0
        for cb in range(2):
            for sc in range(NS):
                k = sc % 2
                T.matmul(pF[0][:], h2p[:, 1 + sc * 128:1 + (sc + 1) * 128], w3[:, cb * 512:(cb + 1) * 512],
                         r=['yh2p', 'yw3'], w=[('ypF', 0)])
                T.matmul(pF[1][:], h2p[:, sc * 128:(sc + 1) * 128], w3[:, 1024 + cb * 512:1024 + (cb + 1) * 512],
                         r=['yh2p', 'yw3'], w=[('ypF', 1)])
                T.act(dec[0][:], dbc[:, cb * 512:(cb + 1) * 512], AF.Exp, scale=tcol[:, sc, 0:1], r=['ydbc', 'ytcol'], w=[('ydec', 0)])
                T.act(dec[1][:], dbc[:, 1024 + cb * 512:1024 + (cb + 1) * 512], AF.Exp, scale=tcol[:, sc, 1:2],
                      r=['ydbc', 'ytcol'], w=[('ydec', 1)])
                T.tt('dve', hf[k][:], pF[0][:], dec[0][:], ALU.mult, r=[('ypF', 0), ('ydec', 0)], w=[('yhf', k)])
                T.tt('dve', hb[k][:], pF[1][:], dec[1][:], ALU.mult, r=[('ypF', 1), ('ydec', 1)], w=[('yhb', k)])
                T.tt('pool', hs[:, sc, :], hf[k][:], hb[k][:], ALU.add, r=[('yhf', k), ('yhb', k)], w=['yhs'])
                T.tt('pool', hd[:, sc, :], hb[k][:], hf[k][:], ALU.subtract, r=[('yhf', k), ('yhb', k)], w=['yhd'])
            for cc in range(4):
                ch = cb * 4 + cc
                for j, blk in enumerate((B_YV, B_YX0, B_YX1)):
                    r0 = blk * 1024 + ch * 128
                    T.dma('sp', raw[j][:], C.d['pxT'][r0:r0 + 128, tok0:tok0 + L], r=[('pxT', 'all')], w=[('yraw', 0)])
                    ci = j * 8 + ch
                    w0_, w1_, w2_ = (cw[:, t_ * 24 + ci:t_ * 24 + ci + 1] for t_ in range(3))
                    T.ts('dve', cv[j][:], raw[j][:], w1_, cb_[:, ci:ci + 1], ALU.mult, ALU.add, r=[('yraw', 0), 'ycw', 'ycb'], w=[('ycv', j)])
                    T.stt('dve', cv[j][:, 1:L], raw[j][:, 0:L - 1], w0_, cv[j][:, 1:L], ALU.mult, ALU.add,
                          r=[('yraw', 0), 'ycw', ('ycv', j)], w=[('ycv', j)])
                    T.stt('dve', cv[j][:, 0:L - 1], raw[j][:, 1:L], w2_, cv[j][:, 0:L - 1], ALU.mult, ALU.add,
                          r=[('yraw', 0), 'ycw', ('ycv', j)], w=[('ycv', j)])
                T.tt('pool', cv[0][:], cv[0][:], cv[2][:], ALU.mult, r=[('ycv', 0), ('ycv', 2)], w=[('ycv', 0)])
                T.dma('sp', C.d['hyvg'][ch * 128:(ch + 1) * 128, tok0:tok0 + L], cv[0][:], r=[('ycv', 0)], w=[('hyvg', 'all')])
                T.dma('sp', C.d['hyx0'][ch * 128:(ch + 1) * 128, tok0:tok0 + L], cv[1][:], r=[('ycv', 1)], w=[('hyx0', 'all')])
                for sc in range(NS):
                    k = sc % 2
                    T.transpose(pF[k][:, 0:128], cv[0][:, sc * 128:(sc + 1) * 128], C.ident[:], r=[('ycv', 0), 'ident'], w=[('ypF', k)])
                    T.copy('act' if sc % 2 else 'dve', vgt[:, sc, cc * 128:(cc + 1) * 128], pF[k][:, 0:128], r=[('ypF', k)], w=['yvgt'])
            Cv = C.d['hy_C' + sfx].rearrange("(sc p) f -> p sc f", p=128)
            Sv = C.d['hy_S' + sfx].rearrange("(sc p) f -> p sc f", p=128)
            for fc in range(NS):
                b = ncm % 2
                ncm += 1
                T.dma('act', cms[b][:], Cv[:, :, fc * 128:(fc + 1) * 128], w=[('ycm', b)])
                T.dma('act', cms[2 + b][:], Sv[:, :, fc * 128:(fc + 1) * 128], w=[('ycm', 2 + b)])
                for (ps_, pkey, mat, mkey, src, skey) in ((pK[0], ('ypK', 0), cms[b], ('ycm', b), hs, 'yhs'),
                                                           (pK[1], ('ypK', 1), cms[2 + b], ('ycm', 2 + b), hd, 'yhd'),
                                                           (pX[0], ('ypX', 0), cms[b], ('ycm', b), vgt, 'yvgt'),
                                                           (pX[1], ('ypX', 1), cms[2 + b], ('ycm', 2 + b), vgt, 'yvgt')):
                    for sc in range(NS):
                        T.matmul(ps_[:], mat[:, sc, :], src[:, sc, :], start=(sc == 0), stop=(sc == NS - 1),
                                 r=[mkey, skey], w=[pkey], sig=(sc == NS - 1))
                k = fc % 2
                T.copy('act', kk[0][:], pK[0][:], r=[('ypK', 0)], w=[('ykk', 0)])
                T.copy('act', kk[1][:], pK[1][:], r=[('ypK', 1)], w=[('ykk', 1)])
                T.tt('dve', t1[0][:], pX[0][:], kk[0][:], ALU.mult, r=[('ypX', 0), ('ykk', 0)], w=[('yt1', 0)])
                T.tt('dve', t2[0][:], pX[1][:], kk[1][:], ALU.mult, r=[('ypX', 1), ('ykk', 1)], w=[('yt2', 0)])
                T.tt('pool', Yre[:, fc, :], t1[0][:], t2[0][:], ALU.add, r=[('yt1', 0), ('yt2', 0)], w=['yYre'])
                T.tt('dve', t1[1][:], pX[0][:], kk[1][:], ALU.mult, r=[('ypX', 0), ('ykk', 1)], w=[('yt1', 1)])
                T.tt('dve', t2[1][:], pX[1][:], kk[0][:], ALU.mult, r=[('ypX', 1), ('ykk', 0)], w=[('yt2', 1)])
                T.tt('pool', Yim[:, fc, :], t1[1][:], t2[1][:], ALU.subtract, r=[('yt1', 1), ('yt2', 1)], w=['yYim'])
            CTv = C.d['hy_CT' + sfx].rearrange("(fc p) s -> p fc s", p=128)
            NSv = C.d['hy_NST' + sfx].rearrange("(fc p) s -> p fc s", p=128)
            ne = 0
            for s0 in range(0, L, SB):
                b = nct % 2
                nct += 1
                T.dma('act', cts[b][:], CTv[:, :, s0:s0 + SB], w=[('yct', b)])
                T.dma('act', cts[2 + b][:], NSv[:, :, s0:s0 + SB], w=[('yct', 2 + b)])
                for cc in range(4):
                    ch = cb * 4 + cc
                    k = ne % 2
                    ne += 1
                    for fc in range(NS):
                        T.matmul(pK[k][:, 0:SB], Yre[:, fc, cc * 128:(cc + 1) * 128], cts[b][:, fc, :], start=(fc == 0), stop=False,
                                 r=['yYre', ('yct', b)], w=[('ypK', k)], sig=False)
                    for fc in range(NS):
                        T.matmul(pK[k][:, 0:SB], Yim[:, fc, cc * 128:(cc + 1) * 128], cts[2 + b][:, fc, :], start=False, stop=(fc == NS - 1),
                                 r=['yYim', ('yct', 2 + b)], w=[('ypK', k)], sig=(fc == NS - 1))
                    T.dma('sp', ev[k][:], C.d['hyvg'][ch * 128:(ch + 1) * 128, tok0 + s0:tok0 + s0 + SB], r=[('hyvg', 'all')], w=[('yev', k)])
                    T.dma('sp', ex[k][:], C.d['hyx0'][ch * 128:(ch + 1) * 128, tok0 + s0:tok0 + s0 + SB], r=[('hyx0', 'all')], w=[('yex', k)])
                    T.ts('dve', ev[k][:], ev[k][:], ybias[:, ch:ch + 1], None, ALU.mult, r=[('yev', k), 'ybias'], w=[('yev', k)])
                    T.stt('dve', ev[k][:], pK[k][:, 0:SB], 1.0 / L, ev[k][:], ALU.mult, ALU.add, r=[('ypK', k), ('yev', k)], w=[('yev', k)])
                    T.tt('pool', eo[k][:], ev[k][:], ex[k][:], ALU.mult, r=[('yev', k), ('yex', k)], w=[('yeo', k)])
                    T.dma('sp', C.d['oT'][3 * 1024 + ch * 128:3 * 1024 + (ch + 1) * 128, tok0 + s0:tok0 + s0 + SB], eo[k][:],
                          r=[('yeo', k)], w=[('oT', 'all')])
        T.flush()


WSPEC = [('w_mod', DEPTH, DM, NMOD, 4), ('ffn_w_in', 2 * DEPTH, DM, 2 * HFF, 1), ('ffn_w_out', 2 * DEPTH, HFF, DM, 1),
         ('w_in', DEPTH, DM, INC, 2), ('w_gate', 4 * DEPTH, DM, DM, 1), ('w_up', 4 * DEPTH, 1024, DM, 1),
         ('w_out', DEPTH, DM, DM, 1)]


def build_program(dbg_ctx=False, layers=DEPTH, nlocal=4):
    nc = bass.Bass("TRN2", target_bir_lowering=False)
    C = Ctx()
    C.nc = nc
    C.d = {}

    def din(name, shape, dt=F32):
        C.d[name] = nc.dram_tensor(name, list(shape), dt, kind="ExternalInput").ap()

    def dout(name, shape, dt=F32):
        C.d[name] = nc.dram_tensor(name, list(shape), dt, kind="ExternalOutput").ap()

    def dscr(name, shape, dt=F32):
        C.d[name] = nc.dram_tensor(name, list(shape), dt).ap()

    din('x_all', [nlocal * NLAT, DM]); din('ctx_all', [nlocal * NCTX, DM]); din('cvec_all', [nlocal * 64, 128]); din('ident', [128, 128])
    din('norm_g', [DEPTH * 96, 128]); din('b_mod', [DEPTH, NMOD])
    C.W = {}
    for (name, nmat, K, N, nb) in WSPEC:
        C.W[name] = []
        for m in range(nmat):
            blocks = []
            for j in range(nb):
                din('%s_%d_%d' % (name, m, j), [K // nb, N])
                blocks.append((j * (K // nb), K // nb, C.d['%s_%d_%d' % (name, m, j)]))
            C.W[name].append(blocks)
    din('blk1', [128, 128]); din('rotm', [128, 128]); din('cosf', [128, NLAT]); din('sinf', [128, NLAT])
    din('attn_qk_g', [DEPTH, 128, 2]); din('attn_subln_g', [DEPTH, 128, 1]); din('attn_lambda', [DEPTH, 1, 256])
    din('hmask', [128, 2, 64]); din('hg_norm_g', [DEPTH, 128, 1]); din('hg_lb_logits', [16, DEPTH, 128])
    din('sel16', [16, 2048]); din('gmask', [128, 4, 128]); din('gdn_conv_w', [DEPTH, 128, 72])
    din('gdn_norm_g', [DEPTH, 128, 1]); din('gdn_prm', [DEPTH, 16, 4])
    din('hy_conv_w', [DEPTH, 128, 72]); din('hy_conv_b', [DEPTH, 128, 24]); din('hy_bias', [DEPTH, 128, 8])
    din('hy_w1', [DEPTH, 33, 64]); din('hy_w2', [DEPTH, 64, 64]); din('hy_w3', [DEPTH, 64, 2048])
    din('hy_vecs', [DEPTH, 64, 4]); din('hy_delta', [1, 2048])
    for sfx, L in (('L', NLAT), ('C', NCTX)):
        din('hy_zT' + sfx, [33, L]); din('hy_tcol' + sfx, [128, L // 128, 2])
        for nm in ('hy_C', 'hy_S', 'hy_CT', 'hy_NST'):
            din(nm + sfx, [L, L], BF16)
    dout('out_all', [nlocal * NLAT, DM])
    if dbg_ctx:
        dout('out_e_all', [nlocal * NCTX, DM])
    dscr('xT', [DM, NTOK]); dscr('xnT', [DM, NTOK], BF16); dscr('actT', [HFF, NTOK], BF16)
    dscr('pxT', [PXR, NTOK]); dscr('oT', [DM, NTOK], BF16); dscr('sgT', [DM, NTOK], BF16)
    dscr('acc32', [DM, NTOK]); dscr('accT', [DM, NTOK], BF16)
    dscr('hyvg', [1024, NTOK]); dscr('hyx0', [1024, NTOK])
    lat_tiles = [t for t in token_tiles() if t[2] == 0]
    with ExitStack() as pst:
        C.pst = pst
        C.T = Trk(nc, pst)
        phase_consts(C)
        for bi in range(nlocal):
            if bi > 0:
                C.T.hard_reset()
                C.T.flush()
            C.d['x'] = C.d['x_all'][bi * NLAT:(bi + 1) * NLAT, :]
            C.d['ctx'] = C.d['ctx_all'][bi * NCTX:(bi + 1) * NCTX, :]
            C.d['cvec'] = C.d['cvec_all'][bi * 64:(bi + 1) * 64, :]
            C.d['out'] = C.d['out_all'][bi * NLAT:(bi + 1) * NLAT, :]
            if dbg_ctx:
                C.d['out_e'] = C.d['out_e_all'][bi * NCTX:(bi + 1) * NCTX, :]
            phase_load_x(C)
            phase_mod(C)
            for l in range(layers):
                last = (l == DEPTH - 1)
                phase_ffn(C, l, 0, 0)
                phase_proj(C, l)
                phase_attn(C, l, not last)
                phase_gdn(C, l, not last)
                phase_hgrn(C, l, not last)
                phase_hyena(C, l, 0, NLAT, 'L')
                if not last:
                    phase_hyena(C, l, NLAT, NCTX, 'C')
                tl = lat_tiles if last else None
                phase_merge(C, l, tl)
                phase_ffn(C, l, 2, 1, tl)
            phase_store_out(C, dbg_ctx=dbg_ctx)
        C.T.final_wait('sp')
        C.T.flush()
    return nc, C.T.nops


def host_constants():
    c = {}
    c['ident'] = np.eye(128, dtype=np.float32)
    b = np.zeros((128, 128), np.float32)
    b[:64, :64] = 1
    b[64:, 64:] = 1
    c['blk1'] = b
    r = np.zeros((128, 128), np.float32)
    for m in range(128):
        if (m % 64) < 32:
            r[m + 32, m] = -1.0
        else:
            r[m - 32, m] = 1.0
    c['rotm'] = r
    inv = (10000.0 ** (-np.arange(16, dtype=np.float32) / 16)).astype(np.float32)
    rows = NLAT // 64
    rr = np.repeat(np.arange(rows, dtype=np.float32), 64)
    cc = np.tile(np.arange(64, dtype=np.float32), rows)
    ang = np.concatenate([rr[:, None] * inv, cc[:, None] * inv], axis=-1).astype(np.float32)
    c['cosf'] = np.ascontiguousarray(np.tile(np.cos(ang).astype(np.float32).T, (4, 1)))
    c['sinf'] = np.ascontiguousarray(np.tile(np.sin(ang).astype(np.float32).T, (4, 1)))
    m0 = (np.arange(64)[None, :] >= np.arange(64)[:, None]).astype(np.float32)
    m1 = (np.arange(64)[None, :] <= np.arange(64)[:, None]).astype(np.float32)
    hm = np.stack([m0, m1], axis=1)
    c['hmask'] = np.ascontiguousarray(np.concatenate([hm, hm], axis=0))
    sel = np.zeros((16, 16, 128), np.float32)
    for q in range(16):
        sel[q, q, :] = 1.0
    c['sel16'] = sel.reshape(16, 2048)
    ii = np.arange(128)[:, None]
    jj = np.arange(128)[None, :]
    same = (ii // 64) == (jj // 64)
    c['gmask'] = np.ascontiguousarray(np.stack([same & (jj < ii), same & (jj > ii), same & (jj <= ii), same & (jj >= ii)],
                                               axis=1).astype(np.float32))
    deltas = np.abs(np.linspace(math.log(1e-2) / 1.5, math.log(1e-2) / 0.3, 1024, dtype=np.float32)).astype(np.float32)
    c['hy_delta'] = np.tile(deltas, 2)[None, :].astype(np.float32)
    bf = ml_dtypes.bfloat16
    for sfx, L in (('L', NLAT), ('C', NCTX)):
        t = np.linspace(0.0, 1.0, L, dtype=np.float32)
        wpos = (2.0 * math.pi * np.arange(L, dtype=np.float32) / L).astype(np.float32)
        fb = np.linspace(1e-4, 15, 16, dtype=np.float32)[None]
        z = np.concatenate([t[:, None], np.cos(fb * wpos[:, None]), -np.sin(fb * wpos[:, None])], axis=-1).astype(np.float32)
        c['hy_zT' + sfx] = np.ascontiguousarray(z.T)
        tm1 = np.concatenate([[0.0], t[:-1]]).astype(np.float32)
        tc = np.stack([-t, -tm1], axis=-1).reshape(L // 128, 128, 2).transpose(1, 0, 2)
        c['hy_tcol' + sfx] = np.ascontiguousarray(tc.astype(np.float32))
        s = np.arange(L, dtype=np.float64)[:, None]
        f = np.arange(L, dtype=np.float64)[None, :]
        th = 2.0 * np.pi * (f + 0.5) * s / (2 * L)
        Cm = np.cos(th)
        Sm = np.sin(th)
        c['hy_C' + sfx] = Cm.astype(np.float32).astype(bf)
        c['hy_S' + sfx] = Sm.astype(np.float32).astype(bf)
        c['hy_CT' + sfx] = np.ascontiguousarray(Cm.T).astype(np.float32).astype(bf)
        c['hy_NST' + sfx] = np.ascontiguousarray(-Sm.T).astype(np.float32).astype(bf)
    return c


def host_params(d):
    f32 = np.float32
    p = {}
    p['norm_g'] = np.ascontiguousarray(d['norm_g'], f32).reshape(DEPTH * 96, 128)
    p['b_mod'] = np.ascontiguousarray(d['b_mod'], f32)
    qk = np.asarray(d['attn_qk_g'], f32)
    p['attn_qk_g'] = np.ascontiguousarray(np.stack([np.tile(qk[:, 0, :], (1, 2)), np.tile(qk[:, 1, :], (1, 2))], axis=-1))
    p['attn_subln_g'] = np.asarray(d['attn_subln_g'], f32).reshape(DEPTH, 128, 1)
    p['attn_lambda'] = np.asarray(d['attn_lambda'], f32).reshape(DEPTH, 1, 256)
    p['hg_norm_g'] = np.asarray(d['hg_norm_g'], f32).reshape(DEPTH, 128, 1)
    p['hg_lb_logits'] = np.ascontiguousarray(np.asarray(d['hg_lb_logits'], f32).reshape(DEPTH, 16, 128).transpose(1, 0, 2))
    p['gdn_conv_w'] = np.ascontiguousarray(np.asarray(d['gdn_conv_w'], f32).reshape(DEPTH, 3, 24, 128).transpose(0, 3, 1, 2).reshape(DEPTH, 128, 72))
    p['gdn_norm_g'] = np.asarray(d['gdn_norm_g'], f32).reshape(DEPTH, 128, 1)
    m0 = np.array([1.0] * 8 + [0.0] * 8, f32)
    p['gdn_prm'] = np.ascontiguousarray(np.stack([np.asarray(d['gdn_a_log'], f32).reshape(DEPTH, 16),
                                                  np.asarray(d['gdn_dt_bias'], f32).reshape(DEPTH, 16),
                                                  np.tile(m0, (DEPTH, 1)), np.tile(1 - m0, (DEPTH, 1))], axis=-1).astype(f32))
    p['hy_conv_w'] = np.ascontiguousarray(np.asarray(d['hy_conv_w'], f32).reshape(DEPTH, 3, 24, 128).transpose(0, 3, 1, 2).reshape(DEPTH, 128, 72))
    p['hy_conv_b'] = np.ascontiguousarray(np.asarray(d['hy_conv_b'], f32).reshape(DEPTH, 24, 128).transpose(0, 2, 1))
    p['hy_bias'] = np.ascontiguousarray(np.asarray(d['hy_bias'], f32).reshape(DEPTH, 8, 128).transpose(0, 2, 1))
    p['hy_w1'] = np.ascontiguousarray(d['hy_w1'], f32)
    p['hy_w2'] = np.ascontiguousarray(d['hy_w2'], f32)
    p['hy_w3'] = np.ascontiguousarray(d['hy_w3'], f32)
    hf = np.asarray(d['hy_freq'], f32)
    p['hy_vecs'] = np.ascontiguousarray(np.stack([np.asarray(d['hy_b1'], f32), np.asarray(d['hy_b2'], f32), hf[:, 0], hf[:, 1]], axis=-1))
    return p


_PROG = {}
NCORES = 8
NLOCAL = 1


def kernel(**inputs):
    if 'nc' not in _PROG:
        _PROG['nc'] = build_program(nlocal=NLOCAL)[0]
    nc = _PROG['nc']
    shared = host_constants()
    shared.update(host_params(inputs))
    for (name, nmat, K, N, nb) in WSPEC:
        w = np.asarray(inputs[name], np.float32).reshape(nmat, nb, K // nb, N)
        for m in range(nmat):
            for j in range(nb):
                shared['%s_%d_%d' % (name, m, j)] = np.ascontiguousarray(w[m, j])
    x = np.asarray(inputs['x'], np.float32)
    c = np.asarray(inputs['c'], np.float32)
    ctx = np.asarray(inputs['ctx'], np.float32)
    c_ctx = np.asarray(inputs['c_ctx'], np.float32)
    in_maps = []
    for k in range(NCORES):
        bs = range(k * NLOCAL, (k + 1) * NLOCAL)
        m = dict(shared)
        m['x_all'] = np.ascontiguousarray(x[k * NLOCAL:(k + 1) * NLOCAL]).reshape(NLOCAL * NLAT, DM)
        m['ctx_all'] = np.ascontiguousarray(ctx[k * NLOCAL:(k + 1) * NLOCAL]).reshape(NLOCAL * NCTX, DM)
        m['cvec_all'] = np.ascontiguousarray(np.concatenate([np.concatenate([c[b], c_ctx]) for b in bs]).reshape(NLOCAL * 64, 128))
        in_maps.append(m)
    res = run_bass_kernel_spmd(nc, in_maps, core_ids=list(range(NCORES)))
    return np.concatenate([np.asarray(r['out_all'], dtype=np.float32).reshape(NLOCAL, NLAT, DM) for r in res.results], axis=0)
```
